# Optimizing a Trainium2 kernel written in Bass

```python
import jax, jax.numpy as jnp
from jax import lax
import numpy as np

D_MODEL = 2048
BATCH = 4
SEQ = 4096
DEPTH = 1
DEC_BATCH = 4
DEC_SEQ = 2048
PAST_LEN = 128

N_HEADS = 4
D_K = D_MODEL // 2
D_V = D_MODEL
HEAD_DK = D_K // N_HEADS
HEAD_DV = D_V // N_HEADS
GATE_RANK = 16
GATE_TAU = 16.0
CHUNK = 64
CONV_DIM = D_MODEL
CONV_WIDTH = 31
CONV_PAD = (CONV_WIDTH - 1) // 2
D_FF = 4 * D_MODEL
EPS = 1e-6

SPLITS = [D_K, D_K, D_V, D_V, GATE_RANK, GATE_RANK, CONV_DIM, CONV_DIM, D_MODEL, D_MODEL]
SPLIT_IDX = list(np.cumsum(SPLITS)[:-1])
D_IN = int(sum(SPLITS))

kernel_name = "hybrid_gla_conformer_encoder"


def rms_norm(x, g):
    xf = x.astype(jnp.float32)
    y = xf * lax.rsqrt(jnp.mean(xf * xf, axis=-1, keepdims=True) + EPS)
    return (y * g.astype(jnp.float32)).astype(x.dtype)


def layer_norm(x, g, b):
    xf = x.astype(jnp.float32)
    mu = jnp.mean(xf, axis=-1, keepdims=True)
    xc = xf - mu
    var = jnp.mean(xc * xc, axis=-1, keepdims=True)
    y = xc * lax.rsqrt(var + EPS) * g.astype(jnp.float32) + b.astype(jnp.float32)
    return y.astype(x.dtype)


def gla_scan(q, k, v, log_a):
    B, H, S, dk = q.shape
    dv = v.shape[-1]
    n = S // CHUNK

    def to_chunks(t):
        return t.reshape(B, H, n, CHUNK, t.shape[-1]).transpose(2, 0, 1, 3, 4)

    qc, kc, vc = to_chunks(q), to_chunks(k), to_chunks(v)
    bc = jnp.cumsum(to_chunks(log_a), axis=-2)
    mask = jnp.tril(jnp.ones((CHUNK, CHUNK), dtype=bool))

    def step(state, inp):
        qi, ki, vi, bi = inp
        b_last = bi[..., -1:, :]
        b_ref = bi[..., CHUNK // 2:CHUNK // 2 + 1, :]
        o_inter = jnp.einsum('bhck,bhkv->bhcv', qi * jnp.exp(bi), state)
        scores = jnp.einsum('bhik,bhjk->bhij', qi * jnp.exp(bi - b_ref), ki * jnp.exp(b_ref - bi))
        scores = jnp.where(mask, scores, 0.0)
        o_intra = jnp.einsum('bhij,bhjv->bhiv', scores, vi)
        k_dec = ki * jnp.exp(b_last - bi)
        new_state = jnp.exp(b_last)[..., 0, :, None] * state + jnp.einsum('bhck,bhcv->bhkv', k_dec, vi)
        return new_state, o_inter + o_intra

    state0 = jnp.zeros((B, H, dk, dv), jnp.float32)
    _, o = lax.scan(step, state0, (qc, kc, vc, bc))
    return o.transpose(1, 2, 0, 3, 4).reshape(B, H, S, dv)


def encoder_layer(x, norm_mix_pre, norm_mix_post, norm_ffn_pre, norm_ffn_post, w_in,
                  w_a_fwd, b_a_fwd, w_a_bwd, b_a_bwd, gla_norm, conv_w, conv_b,
                  conv_ln_g, conv_ln_b, w_pw2, w_out, w_ff1, w_ff2):
    B, S, _ = x.shape
    f32 = jnp.float32
    u = rms_norm(x, norm_mix_pre)
    proj = u @ w_in
    q, k, v, g, z_f, z_b, pw_a, pw_b, gate_a, gate_b = jnp.split(proj, SPLIT_IDX, axis=-1)

    def heads(t, hd):
        return t.reshape(B, S, N_HEADS, hd).transpose(0, 2, 1, 3).astype(f32)

    qh = heads(q, HEAD_DK) * (HEAD_DK ** -0.5)
    kh = heads(k, HEAD_DK)
    vh = heads(v, HEAD_DV)
    la_f = heads(jax.nn.log_sigmoid((z_f @ w_a_fwd + b_a_fwd).astype(f32)) / GATE_TAU, HEAD_DK)
    la_b = heads(jax.nn.log_sigmoid((z_b @ w_a_bwd + b_a_bwd).astype(f32)) / GATE_TAU, HEAD_DK)
    flip = lambda t: jnp.flip(t, axis=2)
    o_f = gla_scan(qh, kh, vh, la_f)
    o_b = flip(gla_scan(flip(qh), flip(kh), flip(vh), flip(la_b)))
    o = o_f + o_b
    o = o * lax.rsqrt(jnp.mean(o * o, axis=-1, keepdims=True) + EPS)
    o = o * gla_norm.astype(f32).reshape(N_HEADS, 1, HEAD_DV)
    out_a = o.transpose(0, 2, 1, 3).reshape(B, S, D_V).astype(x.dtype) * jax.nn.silu(g)

    glu = pw_a * jax.nn.sigmoid(pw_b)
    dw = lax.conv_general_dilated(
        glu, conv_w.astype(glu.dtype)[:, None, :], window_strides=(1,),
        padding=[(CONV_PAD, CONV_PAD)], dimension_numbers=('NWC', 'WIO', 'NWC'),
        feature_group_count=CONV_DIM) + conv_b
    out_b = jax.nn.silu(layer_norm(dw, conv_ln_g, conv_ln_b)) @ w_pw2

    merged = jax.nn.sigmoid(gate_a) * out_a + jax.nn.sigmoid(gate_b) * out_b
    h = x + rms_norm(merged @ w_out, norm_mix_post)

    f = jnp.square(jax.nn.relu(rms_norm(h, norm_ffn_pre) @ w_ff1)) @ w_ff2
    return h + rms_norm(f, norm_ffn_post)


def setup_inputs(seed: int = 0) -> dict:
    key = jax.random.key(seed)
    ks = jax.random.split(key, 24)
    nrm = lambda k, shape, s: jax.random.normal(k, shape, jnp.float32) * s
    gain = lambda k, n: jnp.ones((n,), jnp.float32) + nrm(k, (n,), 0.02)
    return {
        "x_prompt": nrm(ks[0], (BATCH, SEQ, D_MODEL), 1.0),
        "x_sample": nrm(ks[1], (DEC_BATCH, DEC_SEQ, D_MODEL), 1.0),
        "norm_mix_pre": gain(ks[2], D_MODEL),
        "norm_mix_post": gain(ks[3], D_MODEL),
        "norm_ffn_pre": gain(ks[4], D_MODEL),
        "norm_ffn_post": gain(ks[5], D_MODEL),
        "w_in": nrm(ks[6], (D_MODEL, D_IN), D_MODEL ** -0.5),
        "w_a_fwd": nrm(ks[7], (GATE_RANK, D_K), GATE_RANK ** -0.5),
        "b_a_fwd": nrm(ks[8], (D_K,), 0.1),
        "w_a_bwd": nrm(ks[9], (GATE_RANK, D_K), GATE_RANK ** -0.5),
        "b_a_bwd": nrm(ks[10], (D_K,), 0.1),
        "gla_norm": gain(ks[11], D_V),
        "conv_w": nrm(ks[12], (CONV_WIDTH, CONV_DIM), CONV_WIDTH ** -0.5),
        "conv_b": nrm(ks[13], (CONV_DIM,), 0.02),
        "conv_ln_g": gain(ks[14], CONV_DIM),
        "conv_ln_b": nrm(ks[15], (CONV_DIM,), 0.02),
        "w_pw2": nrm(ks[16], (CONV_DIM, D_MODEL), CONV_DIM ** -0.5),
        "w_out": nrm(ks[17], (D_MODEL, D_MODEL), D_MODEL ** -0.5),
        "w_ff1": nrm(ks[18], (D_MODEL, D_FF), D_MODEL ** -0.5),
        "w_ff2": nrm(ks[19], (D_FF, D_MODEL), D_FF ** -0.5),
    }


def reference(x_prompt, x_sample, norm_mix_pre, norm_mix_post, norm_ffn_pre, norm_ffn_post,
              w_in, w_a_fwd, b_a_fwd, w_a_bwd, b_a_bwd, gla_norm, conv_w, conv_b,
              conv_ln_g, conv_ln_b, w_pw2, w_out, w_ff1, w_ff2):
    y_prompt = x_prompt
    y_sample = x_sample
    for _ in range(DEPTH):
        y_prompt = encoder_layer(y_prompt, norm_mix_pre, norm_mix_post, norm_ffn_pre, norm_ffn_post,
                                 w_in, w_a_fwd, b_a_fwd, w_a_bwd, b_a_bwd, gla_norm, conv_w, conv_b,
                                 conv_ln_g, conv_ln_b, w_pw2, w_out, w_ff1, w_ff2)
        y_sample = encoder_layer(y_sample, norm_mix_pre, norm_mix_post, norm_ffn_pre, norm_ffn_post,
                                 w_in, w_a_fwd, b_a_fwd, w_a_bwd, b_a_bwd, gla_norm, conv_w, conv_b,
                                 conv_ln_g, conv_ln_b, w_pw2, w_out, w_ff1, w_ff2)
    return (y_prompt, y_sample)
```

```python
import contextlib
import os
import numpy as np
import concourse.bass as bass
import concourse.mybir as mybir
from concourse.bass_utils import run_bass_kernel_spmd

F32 = mybir.dt.float32
BF16 = mybir.dt.bfloat16
AF = mybir.ActivationFunctionType
ALU = mybir.AluOpType

D = 2048
DK = 1024
DV = 2048
NH = 4
HDK = 256
HDV = 512
DFF = 8192
CW = 31
CP = 15
EPS = 1e-6
D_IN = 14368
OFF_Q, OFF_K, OFF_V, OFF_G = 0, 1024, 2048, 4096
OFF_ZF, OFF_ZB = 6144, 6160
OFF_PA, OFF_PB, OFF_GA, OFF_GB = 6176, 8224, 10272, 12320

COMPUTE = ("pe", "act", "dve", "pool")


class Prog:
    def __init__(self, nc):
        self.nc = nc
        self.streams = {e: [] for e in ("pe", "act", "dve", "pool", "sp")}
        self.semcount = {}
        self.waited = {e: {} for e in self.streams}
        self.last_write = {}
        self.readers = {}

    def _deps(self, eng, reads, writes):
        need = {}

        def add(tok, kind):
            sem, val, deng = tok
            if deng == eng and eng in COMPUTE:
                if eng == "pe" or kind == "war":
                    return
            if self.waited[eng].get(sem, 0) >= val:
                return
            if need.get(sem, 0) < val:
                need[sem] = val

        for k in reads:
            w = self.last_write.get(k)
            if w is not None:
                add(w, "raw")
        for k in writes:
            w = self.last_write.get(k)
            if w is not None:
                add(w, "waw")
            for r in self.readers.get(k, ()):
                add(r, "war")
        return need

    def _record(self, reads, writes, tok):
        for k in reads:
            self.readers.setdefault(k, []).append(tok)
        for k in writes:
            self.last_write[k] = tok
            self.readers[k] = []

    def op(self, eng, fn, reads=(), writes=()):
        need = self._deps(eng, reads, writes)
        st = self.streams[eng]
        for sem, val in need.items():
            st.append(("wait", sem, val))
            self.waited[eng][sem] = val
        sem = "c_" + eng
        val = self.semcount.get(sem, 0) + 1
        self.semcount[sem] = val
        st.append(("ins", fn, sem, 1))
        self._record(reads, writes, (sem, val, eng))

    def dma(self, eng, semkey, fn, reads=(), writes=()):
        need = self._deps(eng, reads, writes)
        st = self.streams[eng]
        sem = "d_" + semkey
        prev = self.semcount.get(sem, 0)
        if prev and self.waited[eng].get(sem, 0) < prev:
            need[sem] = max(need.get(sem, 0), prev)
        for s, val in need.items():
            st.append(("wait", s, val))
            self.waited[eng][s] = val
        val = prev + 16
        self.semcount[sem] = val
        st.append(("ins", fn, sem, 16))
        self._record(reads, writes, (sem, val, "dma"))

    def barrier(self, engines=None, skip=()):
        for eng in (engines or self.streams):
            for sem, val in self.semcount.items():
                if engines is not None and sem.startswith("d_W"):
                    continue
                if any(sem.startswith(p) for p in skip):
                    continue
                if self.waited[eng].get(sem, 0) < val:
                    self.streams[eng].append(("wait", sem, val))
                    self.waited[eng][sem] = val

    def emit(self):
        nc = self.nc
        names = sorted(self.semcount.keys())
        with contextlib.ExitStack() as es:
            sems = {n: es.enter_context(nc.semaphore(n)) for n in names}
            block = es.enter_context(nc.Block())

            def run(engname):
                def body(e):
                    for item in self.streams[engname]:
                        if item[0] == "wait":
                            e.wait_ge(sems[item[1]], item[2])
                        else:
                            item[1](e).then_inc(sems[item[2]], item[3])
                return body

            block.tensor(run("pe"))
            block.scalar(run("act"))
            block.vector(run("dve"))
            block.gpsimd(run("pool"))
            block.sync(run("sp"))


CM_ID, CM_B1, CM_BQ, CM_KD, CM_ONES, CM_MASK = 0, 1, 3, 5, 7, 8
N_CM = 10
REFPOS = 64


def make_cmat():
    j = np.arange(128)[:, None]
    i = np.arange(128)[None, :]
    s = -1.0 / 16.0
    m = np.zeros((N_CM, 128, 128), np.float32)
    m[CM_ID] = np.eye(128)
    m[CM_B1 + 0] = s * (j <= i)
    m[CM_B1 + 1] = s * (j >= i)
    m[CM_BQ + 0] = s * ((j <= i).astype(np.float32) - (j <= REFPOS))
    m[CM_BQ + 1] = s * ((j >= i).astype(np.float32) - (j >= REFPOS))
    m[CM_KD + 0] = s * (j > i)
    m[CM_KD + 1] = s * (j < i)
    m[CM_ONES] = 1.0 / D
    m[CM_MASK + 0] = (j <= i)
    m[CM_MASK + 1] = (j >= i)
    return np.ascontiguousarray(m.transpose(1, 0, 2).reshape(128, N_CM * 128))


PV_GPRE, PV_GFFN, PV_CB, PV_LG, PV_LB, PV_CW = 0, 16, 32, 48, 64, 80
N_PV = 80 + 16 * CW


class _Stop(Exception):
    pass


def build_program(SEQ_P=4096, SEQ_S=2048, NS=4):
    TT = NS * 128
    STOP = os.environ.get("KSTOP", "")

    def stop_at(tag):
        if STOP == tag:
            raise _Stop()
    segs = []
    cb = 0
    for name, S in (("p", SEQ_P), ("s", SEQ_S)):
        T = S // 2
        assert T % TT == 0
        segs.append(dict(name=name, S=S, T=T, cbase=cb))
        cb += T // 128
    NCH = cb

    nc = bass.Bass("TRN2", target_bir_lowering=False)

    def din(name, shape, dt=F32):
        return nc.dram_tensor(name, list(shape), dt, kind="ExternalInput").ap()

    for sg in segs:
        sg["x"] = din("x" + sg["name"], [sg["S"], D])
        sg["y"] = nc.dram_tensor("y" + sg["name"], [sg["T"], D], F32, kind="ExternalOutput").ap()
    w_in = din("w_in", [D, D_IN])
    wz_d = din("wz", [D, 32])
    wa_d = din("wa", [17, 2, DK])
    w_pw2 = din("w_pw2", [D, D])
    w_out = din("w_out", [D, D])
    w_ff1 = din("w_ff1", [D, DFF])
    w_ff2 = din("w_ff2", [DFF, D])
    pvec_d = din("pvec", [128, N_PV])
    rvec_d = din("rvec", [3, D])
    cmat_d = din("cmat", [128, N_CM * 128])
    s2scr = nc.dram_tensor("s2scr", [NCH, NH, 128, 2, HDV], BF16, kind="Internal").ap()
    hscr = nc.dram_tensor("hscr", [NS, 128, D], F32, kind="Internal").ap()
    kvscr = nc.dram_tensor("kvscr", [NCH, 128, DK + DV], BF16, kind="Internal").ap()
    P = Prog(nc)
    es = contextlib.ExitStack()

    def sb(name, shape, dt):
        return es.enter_context(nc.sbuf_tensor(name, list(shape), dt))

    cm = sb("cm", [128, N_CM * 128], F32)
    idb = sb("idb", [128, 128], BF16)
    pv = sb("pv", [128, N_PV], F32)
    gx = sb("gx", [128, 16, 32], F32)
    wa_sb = sb("wa_sb", [64, NH * 512], BF16)
    wz_sb = sb("wz_sb", [128, 16, 48], BF16)
    zT = sb("zT", [64, TT], BF16)
    UT = sb("UT", [128, 16, TT], BF16)
    UH = sb("UH", [128, 16, 32], BF16)
    XT = [sb(f"XT{i}", [128, D], F32) for i in range(2)]
    RB = sb("RB", [128, D], F32)
    ss_t = sb("ss_t", [128, 16], F32)
    rs_t = sb("rs_t", [128, 16], F32)
    Sf = sb("Sf", [128, NH, 2, HDV], F32)
    Sb = [sb("Sb0", [128, NH, 2, HDV], BF16),
          RB[:].bitcast(BF16).rearrange("p (h a b) -> p h a b", h=NH, a=2)]
    ARENA_BYTES = 93184
    arena = sb("arena", [128, ARENA_BYTES // 4], F32)

    class Carver:
        def __init__(self):
            self.off = 0

        def get(self, shape, dt):
            n = int(np.prod(shape))
            nbytes = n * (4 if dt == F32 else 2)
            nbytes = (nbytes + 63) // 64 * 64
            a = self.off
            self.off += nbytes
            assert self.off <= ARENA_BYTES, ("arena overflow", self.off)
            ap = arena[:, a // 4:(a + nbytes) // 4]
            if dt != F32:
                ap = ap.bitcast(dt)
            ap = ap[:, 0:n]
            if len(shape) == 2:
                return ap.rearrange("p (a b) -> p a b", a=shape[0])
            if len(shape) == 1:
                return ap
            raise ValueError

    ps = [es.enter_context(nc.psum_tensor(f"ps{i}", [128, 512], F32)) for i in range(8)]
    psrot = [0]

    PSN = [6]

    def psum(nrot=None):
        nrot = nrot or PSN[0]
        i = psrot[0] % nrot
        psrot[0] += 1
        return i

    evrot = [0]

    def evac_eng():
        evrot[0] += 1
        return "dve" if evrot[0] % 3 == 0 else "act"

    def ecopy(eng, out, in_, reads, writes, scale=None):
        if eng == "act":
            if scale is None:
                P.op("act", lambda e: e.activation(out=out, in_=in_, func=AF.Copy), reads, writes)
            else:
                P.op("act", lambda e: e.activation(out=out, in_=in_, func=AF.Copy, scale=scale), reads, writes)
        else:
            if scale is None:
                P.op(eng, lambda e: e.tensor_copy(out=out, in_=in_), reads, writes)
            else:
                P.op(eng, lambda e: e.tensor_scalar(out=out, in0=in_, scalar1=scale, scalar2=None, op0=ALU.mult),
                     reads, writes)

    NW = 2
    W = [sb(f"W{i}", [128, 16, 512], BF16) for i in range(NW)]
    wrot = [0]

    def blk_pieces():
        L = []
        for j in range(4):
            L.append((("pa", j), [(w_in[:, OFF_PA + j * 512:OFF_PA + (j + 1) * 512], 0)]))
            L.append((("pb", j), [(w_in[:, OFF_PB + j * 512:OFF_PB + (j + 1) * 512], 0)]))
        for cg in range(4):
            L.append((("gb", cg), [(w_in[:, OFF_GB + cg * 512:OFF_GB + (cg + 1) * 512], 0)]))
            L.append((("pw2", cg), [(w_pw2[:, cg * 512:(cg + 1) * 512], 0)]))
        for h in range(NH):
            L.append((("qk", h), [(w_in[:, OFF_Q + h * HDK:OFF_Q + (h + 1) * HDK], 0),
                                  (w_in[:, OFF_K + h * HDK:OFF_K + (h + 1) * HDK], 256)]))
            L.append((("g", h), [(w_in[:, OFF_G + h * HDV:OFF_G + (h + 1) * HDV], 0)]))
            L.append((("ga", h), [(w_in[:, OFF_GA + h * HDV:OFF_GA + (h + 1) * HDV], 0)]))
        for cg in range(4):
            L.append((("wo", cg), [(w_out[:, cg * 512:(cg + 1) * 512], 0)]))
        for kb in range(4):
            for j in range(4):
                L.append((("ff1", kb * 4 + j), [(w_ff1[:, (kb * 4 + j) * 512:(kb * 4 + j + 1) * 512], 0)]))
            for cg in range(4):
                L.append((("ff2", kb, cg), [(w_ff2[kb * 2048:(kb + 1) * 2048, cg * 512:(cg + 1) * 512], 0)]))
        return L

    BLKS = blk_pieces()
    BLK_IDX = {k: i for i, (k, _) in enumerate(BLKS)}
    wscr = nc.dram_tensor("wscr", [len(BLKS), 128, 16, 512], BF16, kind="Internal").ap()

    def convert_weights():
        n = 0
        for key, pieces in BLKS:
            i = BLK_IDX[key]
            for src, c0 in pieces:
                ncol = src.shape[1]
                v = src.rearrange("(kc p) c -> p kc c", p=128)
                P.dma("pool", f"cv{n % 2}", lambda e, v=v, i=i, c0=c0, ncol=ncol: e.dma_start(
                    out=wscr[i, :, :, c0:c0 + ncol], in_=v), writes=[("wscr", i)])
                n += 1

    def wload(key):
        s = wrot[0] % NW
        wrot[0] += 1
        i = BLK_IDX[key]
        P.dma("pool", f"W{s}", lambda e, s=s, i=i: e.dma_start(out=W[s][:], in_=wscr[i, :, :, :]),
              reads=[("wscr", i)], writes=[("W", s)])
        return s

    def mm_group(out_ap, lhs_fn, rhs_fn, reads, writes, nk=16, extra=None):
        def fn(e):
            ins = None
            for kc in range(nk):
                ins = e.matmul(out_ap, lhsT=lhs_fn(kc), rhs=rhs_fn(kc), start=(kc == 0), stop=(kc == nk - 1))
            if extra is not None:
                ins = extra(e)
            return ins
        P.op("pe", fn, reads, writes)

    P.dma("sp", "c0", lambda e: e.dma_start(out=cm[:], in_=cmat_d[:, :]), writes=["cm"])
    P.dma("sp", "c1", lambda e: e.dma_start(out=pv[:], in_=pvec_d[:, :]), writes=["pv"])
    P.op("pool", lambda e: e.memset(wa_sb[:], 0.0), writes=["wa"])
    for d_ in range(2):
        P.dma("pool", "c2", lambda e, d_=d_: e.dma_start(
            out=wa_sb[d_ * 32:d_ * 32 + 17, :].rearrange("p (h c) -> p h c", h=NH)[:, :, d_ * 256:(d_ + 1) * 256],
            in_=wa_d[:, d_, :].rearrange("p (h c) -> p h c", h=NH)), writes=["wa"])
    P.op("pool", lambda e: e.memset(wz_sb[:], 0.0), writes=["wz"])
    for d_ in range(2):
        P.dma("pool", "c3", lambda e, d_=d_: e.dma_start(
            out=wz_sb[:, :, d_ * 32:d_ * 32 + 16],
            in_=wz_d[:, d_ * 16:(d_ + 1) * 16].rearrange("(kc p) c -> p kc c", p=128)), writes=["wz"])
    P.op("dve", lambda e: e.tensor_copy(out=idb[:], in_=cm[:, CM_ID * 128:(CM_ID + 1) * 128]), ["cm"], ["idb"])
    P.op("dve", lambda e: e.memset(zT[:], 1.0), writes=["zT"])
    P.op("dve", lambda e: e.memset(gx[:], 1.0), writes=["gx"])
    for kc in range(16):
        P.op("dve", lambda e, kc=kc: e.tensor_scalar(out=gx[:, kc, :], in0=gx[:, kc, :],
                                                      scalar1=pv[:, PV_GPRE + kc:PV_GPRE + kc + 1], scalar2=None,
                                                      op0=ALU.mult), ["pv", "gx"], ["gx"])

    def CM(b, n=1):
        return cm[:, b * 128:(b + n) * 128]

    xtrot = [0]

    def rstd_from_ss(col, n):
        P.op("act", lambda e: e.activation(out=rs_t[:, col:col + 1], in_=ss_t[:, col:col + 1], func=AF.Ln,
                                           scale=1.0 / n, bias=EPS), [("ss", col)], [("rs", col)])
        P.op("act", lambda e: e.activation(out=rs_t[:, col:col + 1], in_=rs_t[:, col:col + 1], func=AF.Exp,
                                           scale=-0.5), [("rs", col)], [("rs", col)])

    def transposes_to_UT(src_fn, src_keys, gain_col):
        for kc in range(16):
            b = psum()
            pb = ps[b][:].bitcast(BF16)

            def fn(e, kc=kc, pb=pb):
                ins = None
                for st in range(NS):
                    ins = e.transpose(out=pb[:, st * 128:(st + 1) * 128], in_=src_fn(st)[:, kc * 128:(kc + 1) * 128],
                                      identity=idb[:])
                return ins
            P.op("pe", fn, list(src_keys) + ["idb"], [("ps", b)])
            sc = None if gain_col is None else pv[:, gain_col + kc:gain_col + kc + 1]
            ecopy(evac_eng(), UT[:, kc, :], pb[:, 0:TT], [("ps", b), "pv"], [("UT", kc)], scale=sc)

    def x_load(x, row0, XS):
        for st in range(NS):
            xs_ = xtrot[0] % 2
            xtrot[0] += 1
            r0 = row0 + st * 128
            P.dma("sp", f"XT{xs_}", lambda e, xs_=xs_, r0=r0: e.dma_start(out=XT[xs_][:], in_=x[r0:r0 + 128, :]),
                  writes=[("XT", xs_)])
            P.op("act", lambda e, xs_=xs_, st=st: e.activation(out=XS[:, st, :], in_=XT[xs_][:], func=AF.Square,
                                                               accum_out=ss_t[:, st:st + 1]),
                 [("XT", xs_)], [("XS", st), ("ss", st)])
            rstd_from_ss(st, D)
            P.op("dve", lambda e, xs_=xs_, st=st: e.tensor_scalar(out=XS[:, st, :], in0=XT[xs_][:],
                                                                   scalar1=rs_t[:, st:st + 1], scalar2=None,
                                                                   op0=ALU.mult),
                 [("XT", xs_), ("rs", st)], [("XS", st)])

    def presweep():
        cv = Carver()
        Wv = cv.get([16, DV], BF16)
        XS = cv.get([NS, D], BF16)
        kdec = cv.get([DK], BF16)
        vvb = [cv.get([DV], BF16) for _ in range(2)]
        zT2 = cv.get([TT], BF16)
        eb2 = rs_t[:, 8:16]
        WkH = [W[i][:].rearrange("p a b -> p (a b)").rearrange("p (a b) -> p a b", a=8) for i in range(2)]

        class _Wk:
            def __getitem__(self, idx):
                _, kc, cs = idx
                return WkH[kc // 8][:, kc % 8, cs]
        Wk = _Wk()
        for kc4 in range(4):
            src = w_in[kc4 * 512:(kc4 + 1) * 512, OFF_K:OFF_K + DK].rearrange("(kc p) c -> p kc c", p=128)
            P.dma("pool", "pw", lambda e, src=src, kc4=kc4: e.dma_start(
                out=WkH[kc4 // 2][:, (kc4 % 2) * 4:(kc4 % 2) * 4 + 4, :], in_=src), writes=["Wkv"])
            src = w_in[kc4 * 512:(kc4 + 1) * 512, OFF_V:OFF_V + DV].rearrange("(kc p) c -> p kc c", p=128)
            P.dma("pool", "pw", lambda e, src=src, kc4=kc4: e.dma_start(
                out=Wv[:, kc4 * 4:(kc4 + 1) * 4, :], in_=src), writes=["Wkv"])
        convert_weights()
        xt0 = XT[0][:]
        kkb = [xt0[:, 0:512].bitcast(BF16), xt0[:, 512:1024].bitcast(BF16)]
        sp2 = xt0[:, DK:2 * DK]
        zTb = [zT, zT2[0:64, :]]
        P.op("dve", lambda e: e.memset(zT2[:], 1.0), writes=[("zT", 1)])
        utk = [("UT", kc) for kc in range(16)]

        def pre_seg(sg):
            x, T, cbase = sg["x"], sg["T"], sg["cbase"]
            nown = T // 128
            ntile = 2 * T // TT
            sfk = [("Sf", q_) for q_ in range(8)]
            P.op("dve", lambda e: e.memset(Sf[:], 0.0), writes=sfk)
            P.op("dve", lambda e: e.memset(Sb[0][:], 0.0), writes=[("Sb", 0)])
            cur = dict(sb=0)

            def sub_dma(ti, st):
                r0 = ti * TT + st * 128
                P.dma("sp", "XT1", lambda e, r0=r0: e.dma_start(out=XT[1][:], in_=x[r0:r0 + 128, :]),
                      writes=[("XT", 1)])

            def sub_norm(ti, st):
                P.op("act", lambda e, st=st: e.activation(out=XS[:, st, :], in_=XT[1][:], func=AF.Square,
                                                          accum_out=ss_t[:, st:st + 1]),
                     [("XT", 1)], [("XS", st), ("ss", st)])
                rstd_from_ss(st, D)
                P.op("dve", lambda e, st=st: e.tensor_scalar(out=XS[:, st, :], in0=XT[1][:],
                                                              scalar1=rs_t[:, st:st + 1], scalar2=None,
                                                              op0=ALU.mult),
                     [("XT", 1), ("rs", st)], [("XS", st)])

            def tile_load(ti):
                for st in range(NS):
                    sub_dma(ti, st)
                    sub_norm(ti, st)

            def tile_tr(ti):
                transposes_to_UT(lambda st: XS[:, st, :], [("XS", st) for st in range(NS)], PV_GPRE)
                b = psum()
                z = zTb[ti % 2]
                mm_group(ps[b][0:48, 0:TT], lambda kc: wz_sb[:, kc, :], lambda kc: UT[:, kc, :],
                         utk + ["wz"], [("ps", b)])
                ecopy("act", z[32:48, :], ps[b][32:48, 0:TT], [("ps", b)], [("zT", ti % 2)])

            def stage_a1(ti, st, par):
                tok = slice(st * 128, (st + 1) * 128)
                for cg in range(2):
                    b = psum(8)
                    mm_group(ps[b][:, :], lambda kc, tok=tok: UT[:, kc, tok],
                             lambda kc, cg=cg: Wk[:, kc, cg * 512:(cg + 1) * 512], utk + ["Wkv"], [("ps", b)])
                    ecopy(evac_eng(), kkb[par][:, cg * 512:(cg + 1) * 512], ps[b][:, :], [("ps", b)],
                          [("kk", par, cg)])

            def stage_a2(ti, st, par, cgs=(0, 1, 2, 3), store=True):
                tok = slice(st * 128, (st + 1) * 128)
                for cg in cgs:
                    b = psum(8)
                    mm_group(ps[b][:, :], lambda kc, tok=tok: UT[:, kc, tok],
                             lambda kc, cg=cg: Wv[:, kc, cg * 512:(cg + 1) * 512], utk + ["Wkv"], [("ps", b)])
                    ecopy(evac_eng(), vvb[par][:, cg * 512:(cg + 1) * 512], ps[b][:, :], [("ps", b)],
                          [("vv", par, cg)])
                c = ti * NS + st
                if store and c < nown:
                    P.dma("sp", "kvst0", lambda e, c=c, par=par: e.dma_start(
                        out=kvscr[cbase + c, :, 0:DK], in_=kkb[par]),
                        reads=[("kk", par, 0), ("kk", par, 1)], writes=[("kvk", cbase + c)])
                    P.dma("sp", "kvst1", lambda e, c=c, par=par: e.dma_start(
                        out=kvscr[cbase + c, :, DK:DK + DV], in_=vvb[par]),
                        reads=[("vv", par, cg_) for cg_ in range(4)], writes=[("kvv", cbase + c)])

            def stage_b1(ti, st, par):
                c = ti * NS + st
                tok = slice(st * 128, (st + 1) * 128)
                z = zTb[ti % 2]
                zk = ("zT", ti % 2)
                if c > 0:
                    for cg in range(2):
                        b = psum(8)

                        def fy2(e, b=b, tok=tok, cg=cg, z=z):
                            ins = None
                            for hh in range(2):
                                h_ = cg * 2 + hh
                                ins = e.matmul(ps[b][:, hh * 256:(hh + 1) * 256], lhsT=z[0:49, tok],
                                               rhs=wa_sb[0:49, h_ * 512 + 256:(h_ + 1) * 512],
                                               start=True, stop=True)
                            return ins
                        P.op("pe", fy2, [zk, "wa"], [("ps", b)])
                        P.op("act", lambda e, b=b, cg=cg: e.activation(
                            out=sp2[:, cg * 512:(cg + 1) * 512], in_=ps[b][:, :], func=AF.Exp, scale=-1.0),
                            [("ps", b)], [("sp2", cg)])
                        P.op("act", lambda e, cg=cg: e.activation(
                            out=sp2[:, cg * 512:(cg + 1) * 512], in_=sp2[:, cg * 512:(cg + 1) * 512],
                            func=AF.Ln, bias=1.0), [("sp2", cg)], [("sp2", cg)])

            def stage_b2(ti, st, par):
                c = ti * NS + st
                if c > 0:
                    b = psum(8)

                    def fe(e, b=b):
                        ins = None
                        for q in range(8):
                            ins = e.matmul(ps[b][:, q:q + 1], lhsT=sp2[:, q * 128:(q + 1) * 128],
                                           rhs=cm[:, (CM_B1 + 1) * 128:(CM_B1 + 1) * 128 + 1],
                                           start=True, stop=True)
                        return ins
                    P.op("pe", fe, [("sp2", 0), ("sp2", 1), "cm"], [("ps", b)])
                    P.op("act", lambda e, b=b: e.activation(out=eb2, in_=ps[b][:, 0:8], func=AF.Exp),
                         [("ps", b)], ["eb2"])
                    for cg in range(2):
                        b = psum(8)
                        P.op("pe", lambda e, b=b, cg=cg: e.matmul(
                            ps[b][:, :], lhsT=CM(CM_KD + 1), rhs=sp2[:, cg * 512:(cg + 1) * 512],
                            start=True, stop=True), [("sp2", cg), "cm"], [("ps", b)])
                        P.op("act", lambda e, b=b, cg=cg: e.activation(
                            out=sp2[:, cg * 512:(cg + 1) * 512], in_=ps[b][:, :], func=AF.Exp),
                            [("ps", b)], [("sp2", cg)])
                        P.op("dve", lambda e, cg=cg, par=par: e.tensor_tensor(
                            out=kdec[:, cg * 512:(cg + 1) * 512], in0=kkb[par][:, cg * 512:(cg + 1) * 512],
                            in1=sp2[:, cg * 512:(cg + 1) * 512], op=ALU.mult),
                            [("kk", par, cg), ("sp2", cg)], [("kdec", cg)])

            def stage_b3(ti, st, par):
                c = ti * NS + st
                if c < nown:
                    for h in range(NH):
                        P.dma("sp", "s2st", lambda e, h=h, c=c, sbc=cur["sb"]: e.dma_start(
                            out=s2scr[cbase + c, h, :, :, :], in_=Sb[sbc][:, h, :, :]),
                            reads=[("Sb", cur["sb"])], writes=[("s2", cbase + c, h)])
                if c > 0:
                    nxt = 1 - cur["sb"]
                    for h in range(NH):
                        for half in range(2):
                            q = h * 2 + half
                            b = psum(8)
                            P.op("pe", lambda e, b=b, q=q, h=h, par=par: e.matmul(
                                ps[b][:, :], lhsT=kdec[:, q * 128:(q + 1) * 128],
                                rhs=vvb[par][:, h * 512:(h + 1) * 512], start=True, stop=True),
                                [("kdec", q // 4), ("vv", par, h)], [("ps", b)])
                            P.op("dve", lambda e, b=b, q=q, h=h, half=half: e.scalar_tensor_tensor(
                                out=Sf[:, h, half, :], in0=Sf[:, h, half, :], scalar=eb2[:, q:q + 1],
                                in1=ps[b][:, :], op0=ALU.mult, op1=ALU.add),
                                [("ps", b), "eb2", ("Sf", q)], [("Sf", q)])
                    P.op("act", lambda e, nxt=nxt: e.activation(out=Sb[nxt][:], in_=Sf[:], func=AF.Copy),
                         sfk, [("Sb", nxt)])
                    cur["sb"] = nxt

            chunks = [(ti, st) for ti in range(ntile - 1, -1, -1) for st in range(NS - 1, -1, -1)]
            pend = None
            tile_load(ntile - 1)
            for i, (ti, st) in enumerate(chunks):
                if st == NS - 1:
                    tile_tr(ti)
                if ti > 0:
                    sub_dma(ti - 1, NS - 1 - st)
                cur_ = (ti, st, i % 2)
                if pend is not None:
                    stage_b1(*pend)
                stage_a1(*cur_)
                stage_a2(*cur_, cgs=(0, 1), store=False)
                if pend is not None:
                    stage_b2(*pend)
                stage_a2(*cur_, cgs=(2, 3), store=True)
                if pend is not None:
                    stage_b3(*pend)
                if ti > 0:
                    sub_norm(ti - 1, NS - 1 - st)
                pend = cur_
            stage_b1(*pend)
            stage_b2(*pend)
            stage_b3(*pend)

        for sg in segs:
            pre_seg(sg)

    INTILE = ("pe", "act", "dve", "sp")

    def mainsweep():
        cv = Carver()
        XS = cv.get([NS, D], BF16)
        MB = cv.get([NS, D], BF16)
        stage0 = cv.off
        GLU = [cv.get([TT + 2 * CP + 2], BF16) for _ in range(2)]
        NPE = 9
        DG = [cv.get([NPE, 128], BF16) for _ in range(2)]
        ACC = [cv.get([TT], F32) for _ in range(3)]
        ACC2 = [cv.get([TT], F32)] * 2
        SQ = cv.get([TT], F32)
        TS = cv.get([TT], F32)
        TSH = cv.get([32], F32)
        DWB = cv.get([16, TT], BF16)
        MEAN = cv.get([TT], F32)
        RSTD = cv.get([TT], F32)
        NMR = cv.get([TT], F32)
        TMPT = [cv.get([TT], F32)] * 2
        SGB = [cv.get([NS, 512], BF16) for _ in range(4)]
        conv_end = cv.off
        ACTT = XS
        ACTT = XS[:].rearrange("p a b -> p (a b)").rearrange("p (a b) -> p a b", a=16)
        cv.off = stage0
        QT = cv.get([2, TT], F32)
        KT = cv.get([2, TT], F32)
        KK = cv.get([NS, HDK], BF16)
        VV = cv.get([NS, HDV], BF16)
        G1 = cv.get([NS, HDV], BF16)
        G2 = [cv.get([HDV], BF16)] * 2
        GG = G1
        SPX = cv.get([512], F32)
        SP = cv.get([NS, 512], F32)
        EB = [cv.get([512], F32) for _ in range(2)]
        EQ = [cv.get([512], F32) for _ in range(2)]
        EK = [cv.get([512], BF16) for _ in range(2)]
        EKD = [cv.get([HDK], BF16) for _ in range(2)]
        Q1 = [cv.get([2, 256], BF16) for _ in range(2)]
        Q2 = [cv.get([2, 256], BF16) for _ in range(2)]
        K2 = [cv.get([2, 256], BF16) for _ in range(2)]
        KD = [cv.get([HDK], BF16) for _ in range(2)]
        PM = [cv.get([256], BF16) for _ in range(2)]
        OT = [cv.get([HDV], F32)] * 2
        OJ = SPX.bitcast(BF16)[:, 0:HDV]
        S2L = [cv.get([2, HDV], BF16) for _ in range(3)]
        SBT = cv.get([2, HDV], BF16)
        gla_end = cv.off
        cv.off = stage0 - NS * D * 2
        FB = cv.get([NS, D], F32)
        AT = [cv.get([16, TT], BF16) for _ in range(2)]
        RL = [cv.get([TT], BF16) for _ in range(2)]
        ffn_end = cv.off
        utk = [("UT", kc) for kc in range(16)]
        mbk0 = [("MB", st_, q_) for st_ in range(NS) for q_ in range(4)]

        def main_seg(sg):
            x, y, T, cbase = sg["x"], sg["y"], sg["T"], sg["cbase"]
            ntile = T // TT
            P.op("dve", lambda e: e.memset(Sf[:], 0.0), writes=[("Sf", h_) for h_ in range(NH)])
            P.op("dve", lambda e: e.memset(Sb[0][:], 0.0), writes=[("Sb", 0, h_) for h_ in range(NH)])
            def main_tile(ti):
                row0 = ti * TT
                PSN[0] = 6
                if ti == 0:
                    x_load(x, row0, XS)
                transposes_to_UT(lambda st: XS[:, st, :], [("XS", st) for st in range(NS)], PV_GPRE)
                xh_ = xtrot[0] % 2
                xtrot[0] += 1
                XH = XT[xh_][0:32, :]
                XHb = XS[0:32, 0, :]
                xhk = ("XT", xh_)
                if ti > 0:
                    P.dma("sp", f"XT{xh_}", lambda e, row0=row0, XH=XH: e.dma_start(
                        out=XH[0:CP, :], in_=x[row0 - CP:row0, :]), writes=[xhk])
                else:
                    P.op("dve", lambda e, XH=XH: e.memset(XH[:, :], 0.0), writes=[xhk])
                P.dma("sp", f"XT{xh_}", lambda e, row0=row0, XH=XH: e.dma_start(
                    out=XH[CP:2 * CP, :], in_=x[row0 + TT:row0 + TT + CP, :]), writes=[xhk])
                P.op("act", lambda e, XH=XH, XHb=XHb: e.activation(out=XHb, in_=XH, func=AF.Square,
                                                                   accum_out=ss_t[0:32, 8:9]),
                     [xhk], [("XS", 0), ("ss", 8)])
                P.op("act", lambda e: e.activation(out=rs_t[0:32, 8:9], in_=ss_t[0:32, 8:9], func=AF.Ln,
                                                   scale=1.0 / D, bias=EPS), [("ss", 8)], [("rs", 8)])
                P.op("act", lambda e: e.activation(out=rs_t[0:32, 8:9], in_=rs_t[0:32, 8:9], func=AF.Exp,
                                                   scale=-0.5), [("rs", 8)], [("rs", 8)])
                P.op("dve", lambda e, XH=XH, XHb=XHb: e.tensor_scalar(out=XHb, in0=XH, scalar1=rs_t[0:32, 8:9],
                                                                      scalar2=None, op0=ALU.mult),
                     [xhk, ("rs", 8)], [("XS", 0)])
                b = psum()
                pbh = ps[b][:].bitcast(BF16)

                def fth(e, pbh=pbh, XHb=XHb):
                    ins = None
                    for kc in range(16):
                        ins = e.transpose(out=pbh[:, kc * 32:(kc + 1) * 32], in_=XHb[:, kc * 128:(kc + 1) * 128],
                                          identity=idb[0:32, 0:32])
                    return ins
                P.op("pe", fth, [("XS", 0), "idb"], [("ps", b)])
                P.op("dve", lambda e, pbh=pbh: e.tensor_tensor(
                    out=UH[:].rearrange("p a b -> p (a b)"), in0=pbh[:, 0:512],
                    in1=gx[:].rearrange("p a b -> p (a b)"), op=ALU.mult), [("ps", b), "gx"], ["UH"])
                b = psum()
                mm_group(ps[b][0:48, 0:TT], lambda kc: wz_sb[:, kc, :], lambda kc: UT[:, kc, :],
                         utk + ["wz"], [("ps", b)])
                ecopy("dve", zT[0:16, :], ps[b][0:16, 0:TT], [("ps", b)], ["zT"])
                ecopy("act", zT[32:48, :], ps[b][32:48, 0:TT], [("ps", b)], ["zT"])

                stop_at("T1")
                P.barrier(INTILE)
                pending_stats = []
                pending_conv = []

                def stats(c):
                    a = ACC[c % 3]
                    P.op("pe", lambda e, c=c, a=a: e.matmul(ps[6][:, 0:TT], lhsT=CM(CM_ONES), rhs=a,
                                                            start=(c == 0), stop=(c == 15)),
                         [("ACC", c % 3), "cm"], [("ps", 6)])
                    P.op("act", lambda e, a=a: e.activation(out=SQ, in_=a, func=AF.Square),
                         [("ACC", c % 3)], ["SQ"])
                    P.op("pe", lambda e, c=c: e.matmul(ps[7][:, 0:TT], lhsT=CM(CM_ONES), rhs=SQ,
                                                       start=(c == 0), stop=(c == 15)),
                         ["SQ", "cm"], [("ps", 7)])
                    P.op("act", lambda e, c=c, a=a: e.activation(out=DWB[:, c, :], in_=a, func=AF.Copy),
                         [("ACC", c % 3)], [("DWB", c)])

                def gate_b_block(cg):
                    s1 = wload(("gb", cg))
                    sgb = SGB[cg]
                    for st in range(NS):
                        b = psum()
                        tok = slice(st * 128, (st + 1) * 128)
                        mm_group(ps[b][:, :], lambda kc, tok=tok: UT[:, kc, tok], lambda kc, s1=s1: W[s1][:, kc, :],
                                 utk + [("W", s1)], [("ps", b)])
                        P.op("act", lambda e, b=b, sgb=sgb, st=st: e.activation(out=sgb[:, st, :], in_=ps[b][:, :],
                                                                               func=AF.Sigmoid),
                             [("ps", b)], [("SGB", cg, st)])

                for j in range(4):
                    sa = wload(("pa", j))
                    sbk = wload(("pb", j))
                    for q in range(4):
                        c = j * 4 + q
                        g = GLU[c % 2]
                        gk = ("GLU", c % 2)
                        ba, bb, bh = psum(), psum(), psum()
                        cs = slice(q * 128, (q + 1) * 128)
                        mm_group(ps[ba][:, 0:TT], lambda kc, sa=sa, cs=cs: W[sa][:, kc, cs], lambda kc: UT[:, kc, :],
                                 utk + [("W", sa)], [("ps", ba)])
                        mm_group(ps[bb][:, 0:TT], lambda kc, sbk=sbk, cs=cs: W[sbk][:, kc, cs],
                                 lambda kc: UT[:, kc, :], utk + [("W", sbk)], [("ps", bb)])

                        def fh(e, sa=sa, sbk=sbk, cs=cs, bh=bh):
                            ins = None
                            for kc in range(16):
                                ins = e.matmul(ps[bh][:, 0:32], lhsT=W[sa][:, kc, cs], rhs=UH[:, kc, :],
                                               start=(kc == 0), stop=(kc == 15))
                            for kc in range(16):
                                ins = e.matmul(ps[bh][:, 32:64], lhsT=W[sbk][:, kc, cs], rhs=UH[:, kc, :],
                                               start=(kc == 0), stop=(kc == 15))
                            return ins
                        P.op("pe", fh, ["UH", ("W", sa), ("W", sbk)], [("ps", bh)])
                        P.op("act", lambda e, bb=bb: e.activation(out=TS, in_=ps[bb][:, 0:TT], func=AF.Sigmoid),
                             [("ps", bb)], ["TS"])
                        P.op("act", lambda e, bh=bh: e.activation(out=TSH, in_=ps[bh][:, 32:64], func=AF.Sigmoid),
                             [("ps", bh)], ["TSH"])
                        P.op("dve", lambda e, g=g, ba=ba: e.tensor_tensor(out=g[:, CP:CP + TT], in0=ps[ba][:, 0:TT],
                                                                          in1=TS, op=ALU.mult),
                             [("ps", ba), "TS"], [gk])
                        if ti > 0:
                            P.op("dve", lambda e, g=g, bh=bh: e.tensor_tensor(out=g[:, 0:CP], in0=ps[bh][:, 0:CP],
                                                                              in1=TSH[:, 0:CP], op=ALU.mult),
                                 [("ps", bh), "TSH"], [gk])
                        else:
                            P.op("dve", lambda e, g=g: e.memset(g[:, 0:CP], 0.0), [], [gk])
                        P.op("dve", lambda e, g=g, bh=bh: e.tensor_tensor(
                            out=g[:, CP + TT:2 * CP + TT], in0=ps[bh][:, CP:2 * CP], in1=TSH[:, CP:2 * CP],
                            op=ALU.mult), [("ps", bh), "TSH"], [gk])
                        dg = DG[c % 2]
                        dgk = ("DG", c % 2)
                        for k in range(NPE):
                            P.op("act", lambda e, dg=dg, k=k, c=c: e.activation(
                                out=dg[:, k, :], in_=idb[:], func=AF.Copy,
                                scale=pv[:, PV_CW + c * CW + k:PV_CW + c * CW + k + 1]), ["idb", "pv"], [dgk])
                        a = ACC[c % 3]
                        ak = ("ACC", c % 3)
                        a2 = ACC2[0]
                        a2k = ("ACC2", 0)
                        w0, w1 = NPE, NPE + 1
                        P.op("dve", lambda e, g=g, a=a, c=c, w0=w0: e.tensor_scalar(
                            out=a, in0=g[:, w0:w0 + TT], scalar1=pv[:, PV_CW + c * CW + w0:PV_CW + c * CW + w0 + 1],
                            scalar2=pv[:, PV_CB + c:PV_CB + c + 1], op0=ALU.mult, op1=ALU.add),
                            [gk, "pv"], [ak])
                        P.op("dve", lambda e, g=g, a2=a2, c=c, w1=w1: e.tensor_scalar(
                            out=a2, in0=g[:, w1:w1 + TT], scalar1=pv[:, PV_CW + c * CW + w1:PV_CW + c * CW + w1 + 1],
                            scalar2=None, op0=ALU.mult), [gk, "pv"], [a2k])
                        for i_, w in enumerate(range(NPE + 2, CW)):
                            tg, tgk = (a, ak) if i_ % 2 == 0 else (a2, a2k)
                            P.op("dve", lambda e, g=g, tg=tg, c=c, w=w: e.scalar_tensor_tensor(
                                out=tg, in0=g[:, w:w + TT], scalar=pv[:, PV_CW + c * CW + w:PV_CW + c * CW + w + 1],
                                in1=tg, op0=ALU.mult, op1=ALU.add), [gk, tgk, "pv"], [tgk])
                        P.op("dve", lambda e, a=a, a2=a2: e.tensor_tensor(out=a, in0=a, in1=a2, op=ALU.add),
                             [ak, a2k], [ak])
                        def conv_pe(c=c, dg=dg, dgk=dgk, g=g, gk=gk, a=a, ak=ak):
                            bc = psum()

                            def fconv(e):
                                ins = None
                                for k in range(NPE):
                                    ins = e.matmul(ps[bc][:, 0:TT], lhsT=dg[:, k, :], rhs=g[:, k:k + TT],
                                                   start=(k == 0), stop=(k == NPE - 1))
                                return ins
                            P.op("pe", fconv, [dgk, gk], [("ps", bc)])
                            P.op("dve", lambda e: e.tensor_tensor(out=a, in0=ps[bc][:, 0:TT], in1=a, op=ALU.add),
                                 [ak, ("ps", bc)], [ak])
                            pending_stats.append(c)
                        if pending_conv:
                            pending_conv.pop(0)()
                        pending_conv.append(conv_pe)
                        if len(pending_stats) > 1:
                            stats(pending_stats.pop(0))
                while pending_conv:
                    pending_conv.pop(0)()
                while pending_stats:
                    stats(pending_stats.pop(0))
                for cg_ in range(4):
                    gate_b_block(cg_)

                stop_at("T2")
                P.op("dve", lambda e: e.tensor_copy(out=MEAN, in_=ps[6][:, 0:TT]), [("ps", 6)], ["MEAN"])
                P.op("dve", lambda e: e.tensor_tensor(out=NMR, in0=MEAN, in1=MEAN, op=ALU.mult), ["MEAN"], ["NMR"])
                P.op("dve", lambda e: e.tensor_tensor(out=RSTD, in0=ps[7][:, 0:TT], in1=NMR, op=ALU.subtract),
                     [("ps", 7), "NMR"], ["RSTD"])
                P.op("act", lambda e: e.activation(out=RSTD, in_=RSTD, func=AF.Ln, bias=EPS), ["RSTD"], ["RSTD"])
                P.op("act", lambda e: e.activation(out=RSTD, in_=RSTD, func=AF.Exp, scale=-0.5), ["RSTD"], ["RSTD"])
                P.op("dve", lambda e: e.scalar_tensor_tensor(out=NMR, in0=MEAN, scalar=-1.0, in1=RSTD, op0=ALU.mult,
                                                             op1=ALU.mult), ["MEAN", "RSTD"], ["NMR"])
                for c in range(16):
                    t = TMPT[c % 2]
                    tk = ("TMPT", 0)
                    P.op("dve", lambda e, c=c, t=t: e.tensor_tensor(out=t, in0=DWB[:, c, :], in1=RSTD, op=ALU.mult),
                         [("DWB", c), "RSTD"], [tk])
                    P.op("dve", lambda e, t=t: e.tensor_tensor(out=t, in0=t, in1=NMR, op=ALU.add),
                         [tk, "NMR"], [tk])
                    P.op("act", lambda e, c=c, t=t: e.activation(
                        out=ACTT[:, c, :], in_=t, func=AF.Silu, scale=pv[:, PV_LG + c:PV_LG + c + 1],
                        bias=pv[:, PV_LB + c:PV_LB + c + 1]), [tk, "pv"], [("XS", c * NS // 16)])
                actk = [("XS", s_) for s_ in range(NS)]

                stop_at("T3")
                for cg in range(4):
                    sgb = SGB[cg]
                    s2 = wload(("pw2", cg))
                    for st in range(NS):
                        b = psum()
                        tok = slice(st * 128, (st + 1) * 128)
                        mm_group(ps[b][:, :], lambda kc, tok=tok: ACTT[:, kc, tok], lambda kc, s2=s2: W[s2][:, kc, :],
                                 actk + [("W", s2)], [("ps", b)])
                        P.op("dve", lambda e, b=b, sgb=sgb, st=st, cg=cg: e.tensor_tensor(
                            out=MB[:, st, cg * 512:(cg + 1) * 512], in0=ps[b][:, :], in1=sgb[:, st, :], op=ALU.mult),
                            [("ps", b), ("SGB", cg, st)], [("MB", st, cg)])

                stop_at("T4")
                PSN[0] = 8
                P.dma("sp", "rb", lambda e: e.dma_start(out=RB[:], in_=rvec_d[0, :].partition_broadcast(128)),
                      writes=["RB"])
                P.barrier(INTILE)
                def gla_head(h):
                    hk = lambda n: ("H", n)
                    sA = wload(("qk", h))
                    for half in range(2):
                        b = psum()
                        cs = slice(half * 128, (half + 1) * 128)
                        mm_group(ps[b][:, 0:TT], lambda kc, cs=cs: W[sA][:, kc, cs], lambda kc: UT[:, kc, :],
                                 utk + [("W", sA)], [("ps", b)])
                        ecopy(evac_eng(), QT[:, half, :], ps[b][:, 0:TT], [("ps", b)], [hk("QT")], scale=HDK ** -0.5)
                    c0 = cbase + ti * NS
                    P.dma("sp", "kkl", lambda e, c0=c0, h=h: e.dma_start(
                        out=KK[:, :, :], in_=kvscr[c0:c0 + NS, :, h * HDK:(h + 1) * HDK].rearrange("s p c -> p s c")),
                        reads=[("kvk", c0 + i_) for i_ in range(NS)], writes=[hk("KK")])
                    P.dma("sp", "vvl", lambda e, c0=c0, h=h: e.dma_start(
                        out=VV[:, :, :],
                        in_=kvscr[c0:c0 + NS, :, DK + h * HDV:DK + (h + 1) * HDV].rearrange("s p c -> p s c")),
                        reads=[("kvv", c0 + i_) for i_ in range(NS)], writes=[hk("VV")])
                    b = psum()
                    pbk = ps[b][:].bitcast(BF16)

                    def fkt(e, pbk=pbk):
                        ins = None
                        for half in range(2):
                            for st in range(NS):
                                ins = e.transpose(out=pbk[:, half * TT + st * 128:half * TT + (st + 1) * 128],
                                                  in_=KK[:, st, half * 128:(half + 1) * 128], identity=idb[:])
                        return ins
                    P.op("pe", fkt, [hk("KK"), "idb"], [("ps", b)])
                    ecopy("act", KT[:].rearrange("p a b -> p (a b)"), pbk[:, 0:2 * TT], [("ps", b)], [hk("KT")])
                    def emit_gates():
                      sG = wload(("g", h))
                      for st in range(NS):
                          b = psum()
                          tok = slice(st * 128, (st + 1) * 128)
                          mm_group(ps[b][:, :], lambda kc, tok=tok: UT[:, kc, tok], lambda kc: W[sG][:, kc, :],
                                   utk + [("W", sG)], [("ps", b)])
                          P.op("act", lambda e, b=b, st=st: e.activation(out=G1[:, st, :], in_=ps[b][:, :],
                                                                         func=AF.Silu), [("ps", b)], [hk("G1")])
                      sGA = wload(("ga", h))
                      for st in range(NS):
                          b = psum()
                          tok = slice(st * 128, (st + 1) * 128)
                          mm_group(ps[b][:, :], lambda kc, tok=tok: UT[:, kc, tok], lambda kc: W[sGA][:, kc, :],
                                   utk + [("W", sGA)], [("ps", b)])
                          P.op("act", lambda e, b=b, st=st: e.activation(out=G2[st % 2], in_=ps[b][:, :],
                                                                         func=AF.Sigmoid), [("ps", b)],
                               [("G2", 0)])
                          P.op("dve", lambda e, st=st: e.tensor_tensor(out=G1[:, st, :], in0=G1[:, st, :],
                                                                       in1=G2[st % 2], op=ALU.mult),
                               [hk("G1"), ("G2", 0)], [hk("G1")])
                      for st in range(NS):
                          P.op("dve", lambda e, st=st, h=h: e.tensor_tensor(
                              out=GG[:, st, :], in0=G1[:, st, :], in1=RB[:, h * HDV:(h + 1) * HDV], op=ALU.mult),
                              [hk("G1"), "RB"], [hk("G1")])
                    stop_at("T4a")
                    def emit_decays():
                      for st in range(NS):
                          b = psum()
                          tok = slice(st * 128, (st + 1) * 128)

                          P.op("pe", lambda e, b=b, tok=tok, h=h: e.matmul(
                              ps[b][:, :], lhsT=zT[0:49, tok], rhs=wa_sb[0:49, h * 512:(h + 1) * 512],
                              start=True, stop=True), ["zT", "wa"], [("ps", b)])
                          P.op("act", lambda e, b=b: e.activation(out=SPX, in_=ps[b][:, :], func=AF.Exp, scale=-1.0),
                               [("ps", b)], ["SPX"])
                          P.op("act", lambda e, st=st: e.activation(out=SP[:, st, :], in_=SPX, func=AF.Ln, bias=1.0),
                               ["SPX"], [hk("SP")])
                    stop_at("T4b")

                    def chunk_x(st):
                        c = cbase + ti * NS + st
                        r = st % 2
                        ck = lambda n, r=r: ("C", n, r)
                        tok = slice(st * 128, (st + 1) * 128)
                        s2 = (h * NS + st) % 3
                        P.dma("sp", f"s2l{s2}", lambda e, s2=s2, c=c, h=h: e.dma_start(
                            out=S2L[s2][:], in_=s2scr[c, h, :, :, :]), reads=[("s2", c, h)], writes=[("S2L", s2)])
                        bB, bQ, bK = psum(), psum(), psum()

                        def fB(e, bB=bB, bQ=bQ, bK=bK, st=st):
                            ins = None
                            for d in range(2):
                                for half in range(2):
                                    o = (d * 2 + half) * 128
                                    lh = SP[:, st, d * 256 + half * 128:d * 256 + (half + 1) * 128]
                                    e.matmul(ps[bB][:, o:o + 128], lhsT=lh, rhs=CM(CM_B1 + d), start=True, stop=True)
                                    e.matmul(ps[bQ][:, o:o + 128], lhsT=lh, rhs=CM(CM_BQ + d), start=True, stop=True)
                            ins = e.matmul(ps[bK][:, 0:HDK], lhsT=CM(CM_KD + 0), rhs=SP[:, st, 0:HDK],
                                           start=True, stop=True)
                            return ins
                        P.op("pe", fB, [hk("SP"), "cm"], [("ps", bB), ("ps", bQ), ("ps", bK)])
                        P.op("act", lambda e, bB=bB, r=r: e.activation(out=EB[r], in_=ps[bB][:, :], func=AF.Exp),
                             [("ps", bB)], [ck("EB")])
                        P.op("act", lambda e, bQ=bQ, r=r: e.activation(out=EQ[r], in_=ps[bQ][:, :], func=AF.Exp),
                             [("ps", bQ)], [ck("EQ")])
                        P.op("act", lambda e, bQ=bQ, r=r: e.activation(out=EK[r], in_=ps[bQ][:, :], func=AF.Exp,
                                                                       scale=-1.0), [("ps", bQ)], [ck("EK")])
                        P.op("act", lambda e, bK=bK, r=r: e.activation(out=EKD[r], in_=ps[bK][:, 0:HDK],
                                                                       func=AF.Exp), [("ps", bK)], [ck("EKD")])
                        stop_at("T4c")

                    def chunk_x2(st):
                        r = st % 2
                        ck = lambda n, r=r: ("C", n, r)
                        tok = slice(st * 128, (st + 1) * 128)
                        for d in range(2):
                            ev = lambda t, d=d: t[:, d * 256:(d + 1) * 256].rearrange("p (a b) -> p a b", a=2)
                            o3 = lambda t, d=d: t[:, d, :].rearrange("p (a b) -> p a b", a=2)
                            P.op("dve", lambda e, r=r, d=d, ev=ev, o3=o3, tok=tok: e.tensor_tensor(
                                out=o3(Q1[r]), in0=QT[:, :, tok], in1=ev(EB[r]), op=ALU.mult),
                                [hk("QT"), ck("EB")], [ck("Q1")])
                            P.op("dve", lambda e, r=r, d=d, ev=ev, o3=o3, tok=tok: e.tensor_tensor(
                                out=o3(Q2[r]), in0=QT[:, :, tok], in1=ev(EQ[r]), op=ALU.mult),
                                [hk("QT"), ck("EQ")], [ck("Q2")])
                            P.op("dve", lambda e, r=r, d=d, ev=ev, o3=o3, tok=tok: e.tensor_tensor(
                                out=o3(K2[r]), in0=KT[:, :, tok], in1=ev(EK[r]), op=ALU.mult),
                                [hk("KT"), ck("EK")], [ck("K2")])
                        P.op("dve", lambda e, r=r, st=st: e.tensor_tensor(out=KD[r], in0=KK[:, st, :], in1=EKD[r],
                                                                          op=ALU.mult),
                             [hk("KK"), ck("EKD")], [ck("KD")])
                        stop_at("T4d")

                    def chunk_y(st, mid=None):
                        r = st % 2
                        ck = lambda n, r=r: ("C", n, r)
                        s2 = (h * NS + st) % 3
                        assert NS % 2 == 0
                        cur_s, cur_k = (Sb[0][:, h, :, :], ("Sb", 0, h)) if st % 2 == 0 else (SBT, "SBT")
                        nxt_s, nxt_k = (SBT, "SBT") if st % 2 == 0 else (Sb[0][:, h, :, :], ("Sb", 0, h))
                        for half in range(2):
                            b = psum()
                            P.op("pe", lambda e, b=b, r=r, half=half, st=st: e.matmul(
                                ps[b][:, :], lhsT=KD[r][:, half * 128:(half + 1) * 128], rhs=VV[:, st, :],
                                start=True, stop=True), [ck("KD"), hk("VV")], [("ps", b)])
                            P.op("dve", lambda e, b=b, r=r, half=half, h=h: e.scalar_tensor_tensor(
                                out=Sf[:, h, half, :], in0=Sf[:, h, half, :],
                                scalar=EB[r][:, half * 128 + 127:half * 128 + 128], in1=ps[b][:, :],
                                op0=ALU.mult, op1=ALU.add), [("ps", b), ck("EB"), ("Sf", h)], [("Sf", h)])
                        P.op("act", lambda e, h=h, nxt_s=nxt_s: e.activation(out=nxt_s, in_=Sf[:, h, :, :],
                                                                             func=AF.Copy), [("Sf", h)], [nxt_k])
                        bS = psum()

                        def fS(e, bS=bS, r=r):
                            ins = None
                            for d in range(2):
                                for half in range(2):
                                    ins = e.matmul(ps[bS][:, d * 128:(d + 1) * 128],
                                                   lhsT=K2[r][:, d, half * 128:(half + 1) * 128],
                                                   rhs=Q2[r][:, d, half * 128:(half + 1) * 128],
                                                   start=(half == 0), stop=(half == 1))
                            return ins
                        P.op("pe", fS, [ck("K2"), ck("Q2")], [("ps", bS)])
                        P.op("dve", lambda e, bS=bS, r=r: e.tensor_tensor(
                            out=PM[r], in0=ps[bS][:, 0:256], in1=CM(CM_MASK, 2), op=ALU.mult),
                            [("ps", bS), "cm"], [ck("PM")])
                        stop_at("T4e")
                        if mid is not None:
                            mid()
                        bO = psum()

                        def fO(e, bO=bO, r=r, st=st, h=h, s2=s2, cur_s=cur_s):
                            e.matmul(ps[bO][:, :], lhsT=Q1[r][:, 0, 0:128], rhs=cur_s[:, 0, :], start=True, stop=False)
                            e.matmul(ps[bO][:, :], lhsT=Q1[r][:, 0, 128:256], rhs=cur_s[:, 1, :], start=False,
                                     stop=False)
                            e.matmul(ps[bO][:, :], lhsT=PM[r][:, 0:128], rhs=VV[:, st, :], start=False, stop=False)
                            e.matmul(ps[bO][:, :], lhsT=Q1[r][:, 1, 0:128], rhs=S2L[s2][:, 0, :], start=False,
                                     stop=False)
                            e.matmul(ps[bO][:, :], lhsT=Q1[r][:, 1, 128:256], rhs=S2L[s2][:, 1, :], start=False,
                                     stop=False)
                            return e.matmul(ps[bO][:, :], lhsT=PM[r][:, 128:256], rhs=VV[:, st, :], start=False,
                                            stop=True)
                        P.op("pe", fO, [ck("Q1"), ck("PM"), hk("VV"), cur_k, ("S2L", s2)], [("ps", bO)])
                        stop_at("T4f")
                        col = 9 + r
                        P.op("act", lambda e, bO=bO, col=col: e.activation(out=OJ, in_=ps[bO][:, :], func=AF.Square,
                                                                           accum_out=ss_t[:, col:col + 1]),
                             [("ps", bO)], ["SPX", ("ss", col)])
                        rstd_from_ss(col, HDV)
                        P.op("dve", lambda e, bO=bO, col=col, r=r, st=st: e.scalar_tensor_tensor(
                            out=OT[r], in0=ps[bO][:, :], scalar=rs_t[:, col:col + 1], in1=GG[:, st, :],
                            op0=ALU.mult, op1=ALU.mult), [("ps", bO), ("rs", col), hk("G1")], [ck("OT")])
                        P.op("dve", lambda e, r=r, st=st, h=h: e.tensor_tensor(
                            out=MB[:, st, h * HDV:(h + 1) * HDV], in0=MB[:, st, h * HDV:(h + 1) * HDV], in1=OT[r],
                            op=ALU.add), [ck("OT"), ("MB", st, h)], [("MB", st, h)])
                        stop_at("T4g")
                    emit_decays()
                    chunk_x(0)
                    chunk_x2(0)
                    if NS > 1:
                        chunk_x(1)
                        chunk_x2(1)
                    emit_gates()
                    for st in range(NS):
                        chunk_y(st, (lambda st=st: chunk_x(st + 2)) if st + 2 < NS else None)
                        if st + 2 < NS:
                            chunk_x2(st + 2)

                for h_ in range(NH):
                    gla_head(h_)
                stop_at("T5")
                P.barrier(INTILE)
                mbk = [("MB", st, q) for st in range(NS) for q in range(4)]
                transposes_to_UT(lambda st: MB[:, st, :], mbk, None)
                P.dma("sp", "rb", lambda e: e.dma_start(out=RB[:], in_=rvec_d[1, :].partition_broadcast(128)),
                      writes=["RB"])
                for cg in range(4):
                    s = wload(("wo", cg))
                    for st in range(NS):
                        b = psum()
                        tok = slice(st * 128, (st + 1) * 128)
                        mm_group(ps[b][:, :], lambda kc, tok=tok: UT[:, kc, tok], lambda kc, s=s: W[s][:, kc, :],
                                 utk + [("W", s)], [("ps", b)])
                        ecopy(evac_eng(), FB[:, st, cg * 512:(cg + 1) * 512], ps[b][:, :],
                              [("ps", b)], [("FB", st)] + mbk)

                def post_norm_residual(st, res_load, store, final=False):
                    if final:
                        jk = AT[0][:].rearrange("p a b -> p (a b)")[:, 0:D]
                        jkeys = [("AT", 0, i_) for i_ in range(D // TT)]
                        col = 11 + st
                    else:
                        jk, jkeys, col = XS[:, st, :], [("XS", st)], st
                    P.op("act", lambda e, st=st, jk=jk, col=col: e.activation(
                        out=jk, in_=FB[:, st, :], func=AF.Square, accum_out=ss_t[:, col:col + 1]),
                        [("FB", st)], jkeys + [("ss", col)])
                    rstd_from_ss(col, D)
                    xs_ = xtrot[0] % 2
                    xtrot[0] += 1
                    res_load(st, xs_)
                    P.op("dve", lambda e, st=st, col=col: e.scalar_tensor_tensor(
                        out=FB[:, st, :], in0=FB[:, st, :], scalar=rs_t[:, col:col + 1], in1=RB[:], op0=ALU.mult,
                        op1=ALU.mult), [("FB", st), ("rs", col), "RB"], [("FB", st)])
                    P.op("dve", lambda e, st=st, xs_=xs_: e.tensor_tensor(out=FB[:, st, :], in0=FB[:, st, :],
                                                                          in1=XT[xs_][:], op=ALU.add),
                         [("FB", st), ("XT", xs_)], [("FB", st)])
                    store(st)

                def load_x(st, xs_):
                    r0 = row0 + st * 128
                    P.dma("sp", f"XT{xs_}", lambda e, xs_=xs_, r0=r0: e.dma_start(out=XT[xs_][:], in_=x[r0:r0 + 128, :]),
                          writes=[("XT", xs_)])

                def store_h(st):
                    P.dma("sp", "hst", lambda e, st=st: e.dma_start(out=hscr[st, :, :], in_=FB[:, st, :]),
                          reads=[("FB", st)], writes=[("hscr", st)])

                for st in range(NS):
                    post_norm_residual(st, load_x, store_h)

                stop_at("T6")
                for st in range(NS):
                    P.op("act", lambda e, st=st: e.activation(out=XS[:, st, :], in_=FB[:, st, :], func=AF.Square,
                                                              accum_out=ss_t[:, 4 + st:5 + st]),
                         [("FB", st)], [("XS", st), ("ss", 4 + st)])
                    rstd_from_ss(4 + st, D)
                    P.op("dve", lambda e, st=st: e.tensor_scalar(out=XS[:, st, :], in0=FB[:, st, :],
                                                                  scalar1=rs_t[:, 4 + st:5 + st], scalar2=None,
                                                                  op0=ALU.mult),
                         [("FB", st), ("rs", 4 + st)], [("XS", st)])
                transposes_to_UT(lambda st: XS[:, st, :], [("XS", st) for st in range(NS)], PV_GFFN)
                if ti + 1 < ntile:
                    x_load(x, row0 + TT, XS)

                stop_at("T7")
                for kb in range(4):
                    at = AT[kb % 2]
                    for j in range(4):
                        s = wload(("ff1", kb * 4 + j))
                        for q in range(4):
                            b = psum()
                            cs = slice(q * 128, (q + 1) * 128)
                            mm_group(ps[b][:, 0:TT], lambda kc, s=s, cs=cs: W[s][:, kc, cs], lambda kc: UT[:, kc, :],
                                     utk + [("W", s)], [("ps", b)])
                            rl = RL[(j * 4 + q) % 2]
                            rk = ("RL", (j * 4 + q) % 2)
                            P.op("act", lambda e, b=b, rl=rl: e.activation(out=rl, in_=ps[b][:, 0:TT], func=AF.Relu),
                                 [("ps", b)], [rk])
                            P.op("dve", lambda e, rl=rl, at=at, j=j, q=q: e.tensor_tensor(
                                out=at[:, j * 4 + q, :], in0=rl, in1=rl, op=ALU.mult), [rk], [("AT", kb % 2, j * 4 + q)])
                    atk = [("AT", kb % 2, i) for i in range(16)]
                    for cg in range(4):
                        s = wload(("ff2", kb, cg))
                        for st in range(NS):
                            b = psum()
                            tok = slice(st * 128, (st + 1) * 128)
                            mm_group(ps[b][:, :], lambda kc, tok=tok, at=at: at[:, kc, tok],
                                     lambda kc, s=s: W[s][:, kc, :], atk + [("W", s)], [("ps", b)])
                            dst = FB[:, st, cg * 512:(cg + 1) * 512]
                            if kb == 0:
                                ecopy("act", dst, ps[b][:, :], [("ps", b), ("hscr", st)], [("FB", st)])
                            else:
                                P.op("dve", lambda e, b=b, dst=dst: e.tensor_tensor(out=dst, in0=dst, in1=ps[b][:, :],
                                                                                    op=ALU.add),
                                     [("ps", b), ("FB", st)], [("FB", st)])
                P.dma("sp", "rb", lambda e: e.dma_start(out=RB[:], in_=rvec_d[2, :].partition_broadcast(128)),
                      writes=["RB"])

                def load_h(st, xs_):
                    P.dma("sp", f"XT{xs_}", lambda e, xs_=xs_, st=st: e.dma_start(out=XT[xs_][:], in_=hscr[st, :, :]),
                          reads=[("hscr", st)], writes=[("XT", xs_)])

                def store_y(st):
                    r0 = row0 + st * 128
                    P.dma("sp", "yst", lambda e, st=st, r0=r0: e.dma_start(out=y[r0:r0 + 128, :], in_=FB[:, st, :]),
                          reads=[("FB", st)], writes=[("y", sg["name"], r0)])

                for st in range(NS):
                    post_norm_residual(st, load_h, store_y, final=True)

            for ti in range(ntile):
                main_tile(ti)

        for sg in segs:
            main_seg(sg)

    for sg in segs:
        sg["sbcur"] = None
    try:
        stop_at("const")
        presweep()
        P.barrier(skip=("d_cv",))
        stop_at("pre")
        mainsweep()
    except _Stop:
        pass
    P.barrier()
    P.emit()
    es.close()
    return nc


def _pvec(norm_mix_pre, norm_ffn_pre, conv_b, conv_ln_g, conv_ln_b, conv_w_local):
    t = lambda v: np.asarray(v, np.float32).reshape(16, 128).T
    cw = np.asarray(conv_w_local, np.float32).reshape(CW, 16, 128).transpose(2, 1, 0).reshape(128, 16 * CW)
    return np.ascontiguousarray(np.concatenate(
        [t(norm_mix_pre), t(norm_ffn_pre), t(conv_b), t(conv_ln_g), t(conv_ln_b), cw], axis=1))


_NC_CACHE = {}


def kernel(x_prompt, x_sample, norm_mix_pre, norm_mix_post, norm_ffn_pre, norm_ffn_post,
           w_in, w_a_fwd, b_a_fwd, w_a_bwd, b_a_bwd, gla_norm, conv_w, conv_b,
           conv_ln_g, conv_ln_b, w_pw2, w_out, w_ff1, w_ff2, _NS=4):
    x_prompt = np.asarray(x_prompt, np.float32)
    x_sample = np.asarray(x_sample, np.float32)
    B, SP_, _ = x_prompt.shape
    _, SS_, _ = x_sample.shape
    n_cores = 2 * B
    key = (SP_, SS_, _NS)
    if key not in _NC_CACHE:
        _NC_CACHE[key] = build_program(SP_, SS_, _NS)
    nc = _NC_CACHE[key]
    w_in = np.ascontiguousarray(np.asarray(w_in, np.float32))
    cmat = make_cmat()
    rvec = np.ascontiguousarray(np.stack([np.asarray(gla_norm, np.float32), np.asarray(norm_mix_post, np.float32),
                                          np.asarray(norm_ffn_post, np.float32)]))
    zf = w_in[:, OFF_ZF:OFF_ZF + 16]
    zb = w_in[:, OFF_ZB:OFF_ZB + 16]
    waf = np.concatenate([np.asarray(w_a_fwd, np.float32), np.asarray(b_a_fwd, np.float32)[None, :]], 0)
    wab = np.concatenate([np.asarray(w_a_bwd, np.float32), np.asarray(b_a_bwd, np.float32)[None, :]], 0)
    conv_w = np.asarray(conv_w, np.float32)
    shared = dict(w_in=w_in, w_pw2=np.ascontiguousarray(w_pw2, np.float32),
                  w_out=np.ascontiguousarray(w_out, np.float32), w_ff1=np.ascontiguousarray(w_ff1, np.float32),
                  w_ff2=np.ascontiguousarray(w_ff2, np.float32), rvec=rvec, cmat=cmat)
    per_half = []
    for half in range(2):
        if half == 0:
            wz = np.concatenate([zf, zb], 1)
            wa = np.stack([waf, wab], 1)
            cwl = conv_w
        else:
            wz = np.concatenate([zb, zf], 1)
            wa = np.stack([wab, waf], 1)
            cwl = conv_w[::-1]
        per_half.append(dict(wz=np.ascontiguousarray(wz), wa=np.ascontiguousarray(wa),
                             pvec=_pvec(norm_mix_pre, norm_ffn_pre, conv_b, conv_ln_g, conv_ln_b, cwl)))
    in_maps = []
    for c in range(n_cores):
        p, half = c // 2, c % 2
        xp = x_prompt[p] if half == 0 else x_prompt[p][::-1]
        xs = x_sample[p] if half == 0 else x_sample[p][::-1]
        m = dict(shared)
        m.update(per_half[half])
        m["xp"] = np.ascontiguousarray(xp)
        m["xs"] = np.ascontiguousarray(xs)
        in_maps.append(m)
    res = run_bass_kernel_spmd(nc, in_maps, core_ids=list(range(n_cores)))
    y_p = np.empty((B, SP_, D), np.float32)
    y_s = np.empty((B, SS_, D), np.float32)
    for c in range(n_cores):
        p, half = c // 2, c % 2
        r = res.results[c]
        yp, ys = np.asarray(r["yp"]), np.asarray(r["ys"])
        if half == 0:
            y_p[p, :SP_ // 2] = yp
            y_s[p, :SS_ // 2] = ys
        else:
            y_p[p, SP_ // 2:] = yp[::-1]
            y_s[p, SS_ // 2:] = ys[::-1]
    return (y_p, y_s)
```

```python
import contextlib
import os
import numpy as np
import concourse.bass as bass
import concourse.mybir as mybir
from concourse.bass_utils import run_bass_kernel_spmd

F32 = mybir.dt.float32
BF16 = mybir.dt.bfloat16
AF = mybir.ActivationFunctionType
ALU = mybir.AluOpType

D = 2048
DK = 1024
DV = 2048
NH = 4
HDK = 256
HDV = 512
DFF = 8192
CW = 31
CP = 15
EPS = 1e-6
D_IN = 14368
OFF_Q, OFF_K, OFF_V, OFF_G = 0, 1024, 2048, 4096
OFF_ZF, OFF_ZB = 6144, 6160
OFF_PA, OFF_PB, OFF_GA, OFF_GB = 6176, 8224, 10272, 12320

COMPUTE = ("pe", "act", "dve", "pool")


class Prog:
    def __init__(self, nc):
        self.nc = nc
        self.streams = {e: [] for e in ("pe", "act", "dve", "pool", "sp")}
        self.semcount = {}
        self.waited = {e: {} for e in self.streams}
        self.last_write = {}
        self.readers = {}

    def _deps(self, eng, reads, writes):
        need = {}

        def add(tok, kind):
            sem, val, deng = tok
            if deng == eng and eng in COMPUTE:
                if eng == "pe" or kind == "war":
                    return
            if self.waited[eng].get(sem, 0) >= val:
                return
            if need.get(sem, 0) < val:
                need[sem] = val

        for k in reads:
            w = self.last_write.get(k)
            if w is not None:
                add(w, "raw")
        for k in writes:
            w = self.last_write.get(k)
            if w is not None:
                add(w, "waw")
            for r in self.readers.get(k, ()):
                add(r, "war")
        return need

    def _record(self, reads, writes, tok):
        for k in reads:
            self.readers.setdefault(k, []).append(tok)
        for k in writes:
            self.last_write[k] = tok
            self.readers[k] = []

    def op(self, eng, fn, reads=(), writes=()):
        need = self._deps(eng, reads, writes)
        st = self.streams[eng]
        for sem, val in need.items():
            st.append(("wait", sem, val))
            self.waited[eng][sem] = val
        sem = "c_" + eng
        val = self.semcount.get(sem, 0) + 1
        self.semcount[sem] = val
        st.append(("ins", fn, sem, 1))
        self._record(reads, writes, (sem, val, eng))

    def dma(self, eng, semkey, fn, reads=(), writes=()):
        need = self._deps(eng, reads, writes)
        st = self.streams[eng]
        sem = "d_" + semkey
        prev = self.semcount.get(sem, 0)
        if prev and self.waited[eng].get(sem, 0) < prev:
            need[sem] = max(need.get(sem, 0), prev)
        for s, val in need.items():
            st.append(("wait", s, val))
            self.waited[eng][s] = val
        val = prev + 16
        self.semcount[sem] = val
        st.append(("ins", fn, sem, 16))
        self._record(reads, writes, (sem, val, "dma"))

    def barrier(self, engines=None, skip=()):
        for eng in (engines or self.streams):
            for sem, val in self.semcount.items():
                if engines is not None and sem.startswith("d_W"):
                    continue
                if any(sem.startswith(p) for p in skip):
                    continue
                if self.waited[eng].get(sem, 0) < val:
                    self.streams[eng].append(("wait", sem, val))
                    self.waited[eng][sem] = val

    def emit(self):
        nc = self.nc
        names = sorted(self.semcount.keys())
        with contextlib.ExitStack() as es:
            sems = {n: es.enter_context(nc.semaphore(n)) for n in names}
            block = es.enter_context(nc.Block())

            def run(engname):
                def body(e):
                    for item in self.streams[engname]:
                        if item[0] == "wait":
                            e.wait_ge(sems[item[1]], item[2])
                        else:
                            item[1](e).then_inc(sems[item[2]], item[3])
                return body

            block.tensor(run("pe"))
            block.scalar(run("act"))
            block.vector(run("dve"))
            block.gpsimd(run("pool"))
            block.sync(run("sp"))


CM_ID, CM_B1, CM_BQ, CM_KD, CM_ONES, CM_MASK = 0, 1, 3, 5, 7, 8
N_CM = 10
REFPOS = 64


def make_cmat():
    j = np.arange(128)[:, None]
    i = np.arange(128)[None, :]
    s = -1.0 / 16.0
    m = np.zeros((N_CM, 128, 128), np.float32)
    m[CM_ID] = np.eye(128)
    m[CM_B1 + 0] = s * (j <= i)
    m[CM_B1 + 1] = s * (j >= i)
    m[CM_BQ + 0] = s * ((j <= i).astype(np.float32) - (j <= REFPOS))
    m[CM_BQ + 1] = s * ((j >= i).astype(np.float32) - (j >= REFPOS))
    m[CM_KD + 0] = s * (j > i)
    m[CM_KD + 1] = s * (j < i)
    m[CM_ONES] = 1.0 / D
    m[CM_MASK + 0] = (j <= i)
    m[CM_MASK + 1] = (j >= i)
    return np.ascontiguousarray(m.transpose(1, 0, 2).reshape(128, N_CM * 128))


PV_GPRE, PV_GFFN, PV_CB, PV_LG, PV_LB, PV_CW = 0, 16, 32, 48, 64, 80
N_PV = 80 + 16 * CW


class _Stop(Exception):
    pass


def build_program(SEQ_P=4096, SEQ_S=2048, NS=4):
    TT = NS * 128
    STOP = os.environ.get("KSTOP", "")

    def stop_at(tag):
        if STOP == tag:
            raise _Stop()
    segs = []
    cb = 0
    for name, S in (("p", SEQ_P), ("s", SEQ_S)):
        T = S // 2
        assert T % TT == 0
        segs.append(dict(name=name, S=S, T=T, cbase=cb))
        cb += T // 128
    NCH = cb

    nc = bass.Bass("TRN2", target_bir_lowering=False)

    def din(name, shape, dt=F32):
        return nc.dram_tensor(name, list(shape), dt, kind="ExternalInput").ap()

    for sg in segs:
        sg["x"] = din("x" + sg["name"], [sg["S"], D])
        sg["y"] = nc.dram_tensor("y" + sg["name"], [sg["T"], D], F32, kind="ExternalOutput").ap()
    w_in = din("w_in", [D, D_IN])
    wz_d = din("wz", [D, 32])
    wa_d = din("wa", [17, 2, DK])
    w_pw2 = din("w_pw2", [D, D])
    w_out = din("w_out", [D, D])
    w_ff1 = din("w_ff1", [D, DFF])
    w_ff2 = din("w_ff2", [DFF, D])
    pvec_d = din("pvec", [128, N_PV])
    rvec_d = din("rvec", [3, D])
    cmat_d = din("cmat", [128, N_CM * 128])
    s2scr = nc.dram_tensor("s2scr", [NCH, NH, 128, 2, HDV], BF16, kind="Internal").ap()
    hscr = nc.dram_tensor("hscr", [NS, 128, D], F32, kind="Internal").ap()
    kvscr = nc.dram_tensor("kvscr", [NCH, 128, DK + DV], BF16, kind="Internal").ap()
    P = Prog(nc)
    es = contextlib.ExitStack()

    def sb(name, shape, dt):
        return es.enter_context(nc.sbuf_tensor(name, list(shape), dt))

    cm = sb("cm", [128, N_CM * 128], F32)
    idb = sb("idb", [128, 128], BF16)
    pv = sb("pv", [128, N_PV], F32)
    gx = sb("gx", [128, 16, 32], F32)
    wa_sb = sb("wa_sb", [64, NH * 512], BF16)
    wz_sb = sb("wz_sb", [128, 16, 48], BF16)
    zT = sb("zT", [64, TT], BF16)
    UT = sb("UT", [128, 16, TT], BF16)
    UH = sb("UH", [128, 16, 32], BF16)
    XT = [sb(f"XT{i}", [128, D], F32) for i in range(2)]
    RB = sb("RB", [128, D], F32)
    ss_t = sb("ss_t", [128, 16], F32)
    rs_t = sb("rs_t", [128, 16], F32)
    Sf = sb("Sf", [128, NH, 2, HDV], F32)
    Sb = [sb("Sb0", [128, NH, 2, HDV], BF16),
          RB[:].bitcast(BF16).rearrange("p (h a b) -> p h a b", h=NH, a=2)]
    ARENA_BYTES = 93184
    arena = sb("arena", [128, ARENA_BYTES // 4], F32)

    class Carver:
        def __init__(self):
            self.off = 0

        def get(self, shape, dt):
            n = int(np.prod(shape))
            nbytes = n * (4 if dt == F32 else 2)
            nbytes = (nbytes + 63) // 64 * 64
            a = self.off
            self.off += nbytes
            assert self.off <= ARENA_BYTES, ("arena overflow", self.off)
            ap = arena[:, a // 4:(a + nbytes) // 4]
            if dt != F32:
                ap = ap.bitcast(dt)
            ap = ap[:, 0:n]
            if len(shape) == 2:
                return ap.rearrange("p (a b) -> p a b", a=shape[0])
            if len(shape) == 1:
                return ap
            raise ValueError

    ps = [es.enter_context(nc.psum_tensor(f"ps{i}", [128, 512], F32)) for i in range(8)]
    psrot = [0]

    def psum(nrot=6):
        i = psrot[0] % nrot
        psrot[0] += 1
        return i

    evrot = [0]

    def evac_eng():
        evrot[0] += 1
        return "dve" if evrot[0] % 3 == 0 else "act"

    def ecopy(eng, out, in_, reads, writes, scale=None):
        if eng == "act":
            if scale is None:
                P.op("act", lambda e: e.activation(out=out, in_=in_, func=AF.Copy), reads, writes)
            else:
                P.op("act", lambda e: e.activation(out=out, in_=in_, func=AF.Copy, scale=scale), reads, writes)
        else:
            if scale is None:
                P.op(eng, lambda e: e.tensor_copy(out=out, in_=in_), reads, writes)
            else:
                P.op(eng, lambda e: e.tensor_scalar(out=out, in0=in_, scalar1=scale, scalar2=None, op0=ALU.mult),
                     reads, writes)

    NW = 2
    W = [sb(f"W{i}", [128, 16, 512], BF16) for i in range(NW)]
    wrot = [0]

    def blk_pieces():
        L = []
        for j in range(4):
            L.append((("pa", j), [(w_in[:, OFF_PA + j * 512:OFF_PA + (j + 1) * 512], 0)]))
            L.append((("pb", j), [(w_in[:, OFF_PB + j * 512:OFF_PB + (j + 1) * 512], 0)]))
        for cg in range(4):
            L.append((("gb", cg), [(w_in[:, OFF_GB + cg * 512:OFF_GB + (cg + 1) * 512], 0)]))
            L.append((("pw2", cg), [(w_pw2[:, cg * 512:(cg + 1) * 512], 0)]))
        for h in range(NH):
            L.append((("qk", h), [(w_in[:, OFF_Q + h * HDK:OFF_Q + (h + 1) * HDK], 0),
                                  (w_in[:, OFF_K + h * HDK:OFF_K + (h + 1) * HDK], 256)]))
            L.append((("g", h), [(w_in[:, OFF_G + h * HDV:OFF_G + (h + 1) * HDV], 0)]))
            L.append((("ga", h), [(w_in[:, OFF_GA + h * HDV:OFF_GA + (h + 1) * HDV], 0)]))
        for cg in range(4):
            L.append((("wo", cg), [(w_out[:, cg * 512:(cg + 1) * 512], 0)]))
        for kb in range(4):
            for j in range(4):
                L.append((("ff1", kb * 4 + j), [(w_ff1[:, (kb * 4 + j) * 512:(kb * 4 + j + 1) * 512], 0)]))
            for cg in range(4):
                L.append((("ff2", kb, cg), [(w_ff2[kb * 2048:(kb + 1) * 2048, cg * 512:(cg + 1) * 512], 0)]))
        return L

    BLKS = blk_pieces()
    BLK_IDX = {k: i for i, (k, _) in enumerate(BLKS)}
    wscr = nc.dram_tensor("wscr", [len(BLKS), 128, 16, 512], BF16, kind="Internal").ap()

    def convert_weights():
        n = 0
        for key, pieces in BLKS:
            i = BLK_IDX[key]
            for src, c0 in pieces:
                ncol = src.shape[1]
                v = src.rearrange("(kc p) c -> p kc c", p=128)
                P.dma("pool", f"cv{n % 2}", lambda e, v=v, i=i, c0=c0, ncol=ncol: e.dma_start(
                    out=wscr[i, :, :, c0:c0 + ncol], in_=v), writes=[("wscr", i)])
                n += 1

    def wload(key):
        s = wrot[0] % NW
        wrot[0] += 1
        i = BLK_IDX[key]
        P.dma("pool", f"W{s}", lambda e, s=s, i=i: e.dma_start(out=W[s][:], in_=wscr[i, :, :, :]),
              reads=[("wscr", i)], writes=[("W", s)])
        return s

    def mm_group(out_ap, lhs_fn, rhs_fn, reads, writes, nk=16, extra=None):
        def fn(e):
            ins = None
            for kc in range(nk):
                ins = e.matmul(out_ap, lhsT=lhs_fn(kc), rhs=rhs_fn(kc), start=(kc == 0), stop=(kc == nk - 1))
            if extra is not None:
                ins = extra(e)
            return ins
        P.op("pe", fn, reads, writes)

    P.dma("sp", "c0", lambda e: e.dma_start(out=cm[:], in_=cmat_d[:, :]), writes=["cm"])
    P.dma("sp", "c1", lambda e: e.dma_start(out=pv[:], in_=pvec_d[:, :]), writes=["pv"])
    P.op("pool", lambda e: e.memset(wa_sb[:], 0.0), writes=["wa"])
    for d_ in range(2):
        P.dma("pool", "c2", lambda e, d_=d_: e.dma_start(
            out=wa_sb[d_ * 32:d_ * 32 + 17, :].rearrange("p (h c) -> p h c", h=NH)[:, :, d_ * 256:(d_ + 1) * 256],
            in_=wa_d[:, d_, :].rearrange("p (h c) -> p h c", h=NH)), writes=["wa"])
    P.op("pool", lambda e: e.memset(wz_sb[:], 0.0), writes=["wz"])
    for d_ in range(2):
        P.dma("pool", "c3", lambda e, d_=d_: e.dma_start(
            out=wz_sb[:, :, d_ * 32:d_ * 32 + 16],
            in_=wz_d[:, d_ * 16:(d_ + 1) * 16].rearrange("(kc p) c -> p kc c", p=128)), writes=["wz"])
    P.op("dve", lambda e: e.tensor_copy(out=idb[:], in_=cm[:, CM_ID * 128:(CM_ID + 1) * 128]), ["cm"], ["idb"])
    P.op("dve", lambda e: e.memset(zT[:], 1.0), writes=["zT"])
    P.op("dve", lambda e: e.memset(gx[:], 1.0), writes=["gx"])
    for kc in range(16):
        P.op("dve", lambda e, kc=kc: e.tensor_scalar(out=gx[:, kc, :], in0=gx[:, kc, :],
                                                      scalar1=pv[:, PV_GPRE + kc:PV_GPRE + kc + 1], scalar2=None,
                                                      op0=ALU.mult), ["pv", "gx"], ["gx"])

    def CM(b, n=1):
        return cm[:, b * 128:(b + n) * 128]

    xtrot = [0]

    def rstd_from_ss(col, n):
        P.op("act", lambda e: e.activation(out=rs_t[:, col:col + 1], in_=ss_t[:, col:col + 1], func=AF.Ln,
                                           scale=1.0 / n, bias=EPS), [("ss", col)], [("rs", col)])
        P.op("act", lambda e: e.activation(out=rs_t[:, col:col + 1], in_=rs_t[:, col:col + 1], func=AF.Exp,
                                           scale=-0.5), [("rs", col)], [("rs", col)])

    def transposes_to_UT(src_fn, src_keys, gain_col):
        for kc in range(16):
            b = psum()
            pb = ps[b][:].bitcast(BF16)

            def fn(e, kc=kc, pb=pb):
                ins = None
                for st in range(NS):
                    ins = e.transpose(out=pb[:, st * 128:(st + 1) * 128], in_=src_fn(st)[:, kc * 128:(kc + 1) * 128],
                                      identity=idb[:])
                return ins
            P.op("pe", fn, list(src_keys) + ["idb"], [("ps", b)])
            sc = None if gain_col is None else pv[:, gain_col + kc:gain_col + kc + 1]
            ecopy(evac_eng(), UT[:, kc, :], pb[:, 0:TT], [("ps", b), "pv"], [("UT", kc)], scale=sc)

    def x_load(x, row0, XS):
        for st in range(NS):
            xs_ = xtrot[0] % 2
            xtrot[0] += 1
            r0 = row0 + st * 128
            P.dma("sp", f"XT{xs_}", lambda e, xs_=xs_, r0=r0: e.dma_start(out=XT[xs_][:], in_=x[r0:r0 + 128, :]),
                  writes=[("XT", xs_)])
            P.op("act", lambda e, xs_=xs_, st=st: e.activation(out=XS[:, st, :], in_=XT[xs_][:], func=AF.Square,
                                                               accum_out=ss_t[:, st:st + 1]),
                 [("XT", xs_)], [("XS", st), ("ss", st)])
            rstd_from_ss(st, D)
            P.op("dve", lambda e, xs_=xs_, st=st: e.tensor_scalar(out=XS[:, st, :], in0=XT[xs_][:],
                                                                   scalar1=rs_t[:, st:st + 1], scalar2=None,
                                                                   op0=ALU.mult),
                 [("XT", xs_), ("rs", st)], [("XS", st)])

    def presweep():
        cv = Carver()
        Wv = cv.get([16, DV], BF16)
        XS = cv.get([NS, D], BF16)
        kdec = cv.get([DK], BF16)
        vvb = [cv.get([DV], BF16) for _ in range(2)]
        zT2 = cv.get([TT], BF16)
        eb2 = rs_t[:, 8:16]
        WkH = [W[i][:].rearrange("p a b -> p (a b)").rearrange("p (a b) -> p a b", a=8) for i in range(2)]

        class _Wk:
            def __getitem__(self, idx):
                _, kc, cs = idx
                return WkH[kc // 8][:, kc % 8, cs]
        Wk = _Wk()
        for kc4 in range(4):
            src = w_in[kc4 * 512:(kc4 + 1) * 512, OFF_K:OFF_K + DK].rearrange("(kc p) c -> p kc c", p=128)
            P.dma("pool", "pw", lambda e, src=src, kc4=kc4: e.dma_start(
                out=WkH[kc4 // 2][:, (kc4 % 2) * 4:(kc4 % 2) * 4 + 4, :], in_=src), writes=["Wkv"])
            src = w_in[kc4 * 512:(kc4 + 1) * 512, OFF_V:OFF_V + DV].rearrange("(kc p) c -> p kc c", p=128)
            P.dma("pool", "pw", lambda e, src=src, kc4=kc4: e.dma_start(
                out=Wv[:, kc4 * 4:(kc4 + 1) * 4, :], in_=src), writes=["Wkv"])
        convert_weights()
        xt0 = XT[0][:]
        kkb = [xt0[:, 0:512].bitcast(BF16), xt0[:, 512:1024].bitcast(BF16)]
        sp2 = xt0[:, DK:2 * DK]
        zTb = [zT, zT2[0:64, :]]
        P.op("dve", lambda e: e.memset(zT2[:], 1.0), writes=[("zT", 1)])
        utk = [("UT", kc) for kc in range(16)]

        def pre_seg(sg):
            x, T, cbase = sg["x"], sg["T"], sg["cbase"]
            nown = T // 128
            ntile = 2 * T // TT
            sfk = [("Sf", q_) for q_ in range(8)]
            P.op("dve", lambda e: e.memset(Sf[:], 0.0), writes=sfk)
            P.op("dve", lambda e: e.memset(Sb[0][:], 0.0), writes=[("Sb", 0)])
            cur = dict(sb=0)

            def sub_dma(ti, st):
                r0 = ti * TT + st * 128
                P.dma("sp", "XT1", lambda e, r0=r0: e.dma_start(out=XT[1][:], in_=x[r0:r0 + 128, :]),
                      writes=[("XT", 1)])

            def sub_norm(ti, st):
                P.op("act", lambda e, st=st: e.activation(out=XS[:, st, :], in_=XT[1][:], func=AF.Square,
                                                          accum_out=ss_t[:, st:st + 1]),
                     [("XT", 1)], [("XS", st), ("ss", st)])
                rstd_from_ss(st, D)
                P.op("dve", lambda e, st=st: e.tensor_scalar(out=XS[:, st, :], in0=XT[1][:],
                                                              scalar1=rs_t[:, st:st + 1], scalar2=None,
                                                              op0=ALU.mult),
                     [("XT", 1), ("rs", st)], [("XS", st)])

            def tile_load(ti):
                for st in range(NS):
                    sub_dma(ti, st)
                    sub_norm(ti, st)

            def tile_tr(ti):
                transposes_to_UT(lambda st: XS[:, st, :], [("XS", st) for st in range(NS)], PV_GPRE)
                b = psum()
                z = zTb[ti % 2]
                mm_group(ps[b][0:48, 0:TT], lambda kc: wz_sb[:, kc, :], lambda kc: UT[:, kc, :],
                         utk + ["wz"], [("ps", b)])
                ecopy("act", z[32:48, :], ps[b][32:48, 0:TT], [("ps", b)], [("zT", ti % 2)])

            def stage_a1(ti, st, par):
                tok = slice(st * 128, (st + 1) * 128)
                for cg in range(2):
                    b = psum(8)
                    mm_group(ps[b][:, :], lambda kc, tok=tok: UT[:, kc, tok],
                             lambda kc, cg=cg: Wk[:, kc, cg * 512:(cg + 1) * 512], utk + ["Wkv"], [("ps", b)])
                    ecopy(evac_eng(), kkb[par][:, cg * 512:(cg + 1) * 512], ps[b][:, :], [("ps", b)],
                          [("kk", par, cg)])

            def stage_a2(ti, st, par, cgs=(0, 1, 2, 3), store=True):
                tok = slice(st * 128, (st + 1) * 128)
                for cg in cgs:
                    b = psum(8)
                    mm_group(ps[b][:, :], lambda kc, tok=tok: UT[:, kc, tok],
                             lambda kc, cg=cg: Wv[:, kc, cg * 512:(cg + 1) * 512], utk + ["Wkv"], [("ps", b)])
                    ecopy(evac_eng(), vvb[par][:, cg * 512:(cg + 1) * 512], ps[b][:, :], [("ps", b)],
                          [("vv", par, cg)])
                c = ti * NS + st
                if store and c < nown:
                    P.dma("sp", "kvst0", lambda e, c=c, par=par: e.dma_start(
                        out=kvscr[cbase + c, :, 0:DK], in_=kkb[par]),
                        reads=[("kk", par, 0), ("kk", par, 1)], writes=[("kvk", cbase + c)])
                    P.dma("sp", "kvst1", lambda e, c=c, par=par: e.dma_start(
                        out=kvscr[cbase + c, :, DK:DK + DV], in_=vvb[par]),
                        reads=[("vv", par, cg_) for cg_ in range(4)], writes=[("kvv", cbase + c)])

            def stage_b1(ti, st, par):
                c = ti * NS + st
                tok = slice(st * 128, (st + 1) * 128)
                z = zTb[ti % 2]
                zk = ("zT", ti % 2)
                if c > 0:
                    for cg in range(2):
                        b = psum(8)

                        def fy2(e, b=b, tok=tok, cg=cg, z=z):
                            ins = None
                            for hh in range(2):
                                h_ = cg * 2 + hh
                                ins = e.matmul(ps[b][:, hh * 256:(hh + 1) * 256], lhsT=z[0:49, tok],
                                               rhs=wa_sb[0:49, h_ * 512 + 256:(h_ + 1) * 512],
                                               start=True, stop=True)
                            return ins
                        P.op("pe", fy2, [zk, "wa"], [("ps", b)])
                        P.op("act", lambda e, b=b, cg=cg: e.activation(
                            out=sp2[:, cg * 512:(cg + 1) * 512], in_=ps[b][:, :], func=AF.Exp, scale=-1.0),
                            [("ps", b)], [("sp2", cg)])
                        P.op("act", lambda e, cg=cg: e.activation(
                            out=sp2[:, cg * 512:(cg + 1) * 512], in_=sp2[:, cg * 512:(cg + 1) * 512],
                            func=AF.Ln, bias=1.0), [("sp2", cg)], [("sp2", cg)])

            def stage_b2(ti, st, par):
                c = ti * NS + st
                if c > 0:
                    b = psum(8)

                    def fe(e, b=b):
                        ins = None
                        for q in range(8):
                            ins = e.matmul(ps[b][:, q:q + 1], lhsT=sp2[:, q * 128:(q + 1) * 128],
                                           rhs=cm[:, (CM_B1 + 1) * 128:(CM_B1 + 1) * 128 + 1],
                                           start=True, stop=True)
                        return ins
                    P.op("pe", fe, [("sp2", 0), ("sp2", 1), "cm"], [("ps", b)])
                    P.op("act", lambda e, b=b: e.activation(out=eb2, in_=ps[b][:, 0:8], func=AF.Exp),
                         [("ps", b)], ["eb2"])
                    for cg in range(2):
                        b = psum(8)
                        P.op("pe", lambda e, b=b, cg=cg: e.matmul(
                            ps[b][:, :], lhsT=CM(CM_KD + 1), rhs=sp2[:, cg * 512:(cg + 1) * 512],
                            start=True, stop=True), [("sp2", cg), "cm"], [("ps", b)])
                        P.op("act", lambda e, b=b, cg=cg: e.activation(
                            out=sp2[:, cg * 512:(cg + 1) * 512], in_=ps[b][:, :], func=AF.Exp),
                            [("ps", b)], [("sp2", cg)])
                        P.op("dve", lambda e, cg=cg, par=par: e.tensor_tensor(
                            out=kdec[:, cg * 512:(cg + 1) * 512], in0=kkb[par][:, cg * 512:(cg + 1) * 512],
                            in1=sp2[:, cg * 512:(cg + 1) * 512], op=ALU.mult),
                            [("kk", par, cg), ("sp2", cg)], [("kdec", cg)])

            def stage_b3(ti, st, par):
                c = ti * NS + st
                if c < nown:
                    for h in range(NH):
                        P.dma("sp", "s2st", lambda e, h=h, c=c, sbc=cur["sb"]: e.dma_start(
                            out=s2scr[cbase + c, h, :, :, :], in_=Sb[sbc][:, h, :, :]),
                            reads=[("Sb", cur["sb"])], writes=[("s2", cbase + c, h)])
                if c > 0:
                    nxt = 1 - cur["sb"]
                    for h in range(NH):
                        for half in range(2):
                            q = h * 2 + half
                            b = psum(8)
                            P.op("pe", lambda e, b=b, q=q, h=h, par=par: e.matmul(
                                ps[b][:, :], lhsT=kdec[:, q * 128:(q + 1) * 128],
                                rhs=vvb[par][:, h * 512:(h + 1) * 512], start=True, stop=True),
                                [("kdec", q // 4), ("vv", par, h)], [("ps", b)])
                            P.op("dve", lambda e, b=b, q=q, h=h, half=half: e.scalar_tensor_tensor(
                                out=Sf[:, h, half, :], in0=Sf[:, h, half, :], scalar=eb2[:, q:q + 1],
                                in1=ps[b][:, :], op0=ALU.mult, op1=ALU.add),
                                [("ps", b), "eb2", ("Sf", q)], [("Sf", q)])
                    P.op("act", lambda e, nxt=nxt: e.activation(out=Sb[nxt][:], in_=Sf[:], func=AF.Copy),
                         sfk, [("Sb", nxt)])
                    cur["sb"] = nxt

            chunks = [(ti, st) for ti in range(ntile - 1, -1, -1) for st in range(NS - 1, -1, -1)]
            pend = None
            tile_load(ntile - 1)
            for i, (ti, st) in enumerate(chunks):
                if st == NS - 1:
                    tile_tr(ti)
                if ti > 0:
                    sub_dma(ti - 1, NS - 1 - st)
                cur_ = (ti, st, i % 2)
                if pend is not None:
                    stage_b1(*pend)
                stage_a1(*cur_)
                stage_a2(*cur_, cgs=(0, 1), store=False)
                if pend is not None:
                    stage_b2(*pend)
                stage_a2(*cur_, cgs=(2, 3), store=True)
                if pend is not None:
                    stage_b3(*pend)
                if ti > 0:
                    sub_norm(ti - 1, NS - 1 - st)
                pend = cur_
            stage_b1(*pend)
            stage_b2(*pend)
            stage_b3(*pend)

        for sg in segs:
            pre_seg(sg)

    INTILE = ("act", "dve", "sp")

    def mainsweep():
        cv = Carver()
        XS = cv.get([NS, D], BF16)
        MB = cv.get([NS, D], BF16)
        stage0 = cv.off
        GLU = [cv.get([TT + 2 * CP + 2], BF16) for _ in range(2)]
        NPE = 9
        DG = [cv.get([NPE, 128], BF16) for _ in range(2)]
        ACC = [cv.get([TT], F32) for _ in range(3)]
        ACC2 = [cv.get([TT], F32)] * 2
        SQ = cv.get([TT], F32)
        TS = cv.get([TT], F32)
        TSH = cv.get([32], F32)
        DWB = cv.get([16, TT], BF16)
        MEAN = cv.get([TT], F32)
        RSTD = cv.get([TT], F32)
        NMR = cv.get([TT], F32)
        TMPT = [cv.get([TT], F32)] * 2
        SGB = [cv.get([NS, 512], BF16) for _ in range(4)]
        conv_end = cv.off
        ACTT = XS
        ACTT = XS[:].rearrange("p a b -> p (a b)").rearrange("p (a b) -> p a b", a=16)
        cv.off = stage0
        QT = cv.get([2, TT], F32)
        KT = cv.get([2, TT], F32)
        KK = cv.get([NS, HDK], BF16)
        VV = cv.get([NS, HDV], BF16)
        G1 = cv.get([NS, HDV], BF16)
        G2 = [cv.get([HDV], BF16)] * 2
        GG = G1
        SPX = cv.get([512], F32)
        SP = cv.get([NS, 512], F32)
        EB = [cv.get([512], F32) for _ in range(2)]
        EQ = [cv.get([512], F32) for _ in range(2)]
        EK = [cv.get([512], BF16) for _ in range(2)]
        EKD = [cv.get([HDK], BF16) for _ in range(2)]
        Q1 = [cv.get([2, 256], BF16) for _ in range(2)]
        Q2 = [cv.get([2, 256], BF16) for _ in range(2)]
        K2 = [cv.get([2, 256], BF16) for _ in range(2)]
        KD = [cv.get([HDK], BF16) for _ in range(2)]
        PM = [cv.get([256], BF16) for _ in range(2)]
        OT = [cv.get([HDV], F32)] * 2
        OJ = SPX.bitcast(BF16)[:, 0:HDV]
        S2L = [cv.get([2, HDV], BF16) for _ in range(3)]
        SBT = cv.get([2, HDV], BF16)
        gla_end = cv.off
        cv.off = stage0 - NS * D * 2
        FB = cv.get([NS, D], F32)
        AT = [cv.get([16, TT], BF16) for _ in range(2)]
        RL = [cv.get([TT], BF16) for _ in range(2)]
        ffn_end = cv.off
        utk = [("UT", kc) for kc in range(16)]
        mbk0 = [("MB", st_, q_) for st_ in range(NS) for q_ in range(4)]

        def main_seg(sg):
            x, y, T, cbase = sg["x"], sg["y"], sg["T"], sg["cbase"]
            ntile = T // TT
            P.op("dve", lambda e: e.memset(Sf[:], 0.0), writes=[("Sf", h_) for h_ in range(NH)])
            P.op("dve", lambda e: e.memset(Sb[0][:], 0.0), writes=[("Sb", 0, h_) for h_ in range(NH)])
            def main_tile(ti):
                row0 = ti * TT
                if ti == 0:
                    x_load(x, row0, XS)
                transposes_to_UT(lambda st: XS[:, st, :], [("XS", st) for st in range(NS)], PV_GPRE)
                xh_ = xtrot[0] % 2
                xtrot[0] += 1
                XH = XT[xh_][0:32, :]
                XHb = XS[0:32, 0, :]
                xhk = ("XT", xh_)
                if ti > 0:
                    P.dma("sp", f"XT{xh_}", lambda e, row0=row0, XH=XH: e.dma_start(
                        out=XH[0:CP, :], in_=x[row0 - CP:row0, :]), writes=[xhk])
                else:
                    P.op("dve", lambda e, XH=XH: e.memset(XH[:, :], 0.0), writes=[xhk])
                P.dma("sp", f"XT{xh_}", lambda e, row0=row0, XH=XH: e.dma_start(
                    out=XH[CP:2 * CP, :], in_=x[row0 + TT:row0 + TT + CP, :]), writes=[xhk])
                P.op("act", lambda e, XH=XH, XHb=XHb: e.activation(out=XHb, in_=XH, func=AF.Square,
                                                                   accum_out=ss_t[0:32, 8:9]),
                     [xhk], [("XS", 0), ("ss", 8)])
                P.op("act", lambda e: e.activation(out=rs_t[0:32, 8:9], in_=ss_t[0:32, 8:9], func=AF.Ln,
                                                   scale=1.0 / D, bias=EPS), [("ss", 8)], [("rs", 8)])
                P.op("act", lambda e: e.activation(out=rs_t[0:32, 8:9], in_=rs_t[0:32, 8:9], func=AF.Exp,
                                                   scale=-0.5), [("rs", 8)], [("rs", 8)])
                P.op("dve", lambda e, XH=XH, XHb=XHb: e.tensor_scalar(out=XHb, in0=XH, scalar1=rs_t[0:32, 8:9],
                                                                      scalar2=None, op0=ALU.mult),
                     [xhk, ("rs", 8)], [("XS", 0)])
                b = psum()
                pbh = ps[b][:].bitcast(BF16)

                def fth(e, pbh=pbh, XHb=XHb):
                    ins = None
                    for kc in range(16):
                        ins = e.transpose(out=pbh[:, kc * 32:(kc + 1) * 32], in_=XHb[:, kc * 128:(kc + 1) * 128],
                                          identity=idb[0:32, 0:32])
                    return ins
                P.op("pe", fth, [("XS", 0), "idb"], [("ps", b)])
                P.op("dve", lambda e, pbh=pbh: e.tensor_tensor(
                    out=UH[:].rearrange("p a b -> p (a b)"), in0=pbh[:, 0:512],
                    in1=gx[:].rearrange("p a b -> p (a b)"), op=ALU.mult), [("ps", b), "gx"], ["UH"])
                b = psum()
                mm_group(ps[b][0:48, 0:TT], lambda kc: wz_sb[:, kc, :], lambda kc: UT[:, kc, :],
                         utk + ["wz"], [("ps", b)])
                ecopy("dve", zT[0:16, :], ps[b][0:16, 0:TT], [("ps", b)], ["zT"])
                ecopy("act", zT[32:48, :], ps[b][32:48, 0:TT], [("ps", b)], ["zT"])

                stop_at("T1")
                P.barrier(INTILE)
                pending_stats = []
                pending_conv = []

                def stats(c):
                    a = ACC[c % 3]
                    P.op("pe", lambda e, c=c, a=a: e.matmul(ps[6][:, 0:TT], lhsT=CM(CM_ONES), rhs=a,
                                                            start=(c == 0), stop=(c == 15)),
                         [("ACC", c % 3), "cm"], [("ps", 6)])
                    P.op("act", lambda e, a=a: e.activation(out=SQ, in_=a, func=AF.Square),
                         [("ACC", c % 3)], ["SQ"])
                    P.op("pe", lambda e, c=c: e.matmul(ps[7][:, 0:TT], lhsT=CM(CM_ONES), rhs=SQ,
                                                       start=(c == 0), stop=(c == 15)),
                         ["SQ", "cm"], [("ps", 7)])
                    P.op("act", lambda e, c=c, a=a: e.activation(out=DWB[:, c, :], in_=a, func=AF.Copy),
                         [("ACC", c % 3)], [("DWB", c)])

                def gate_b_block(cg):
                    s1 = wload(("gb", cg))
                    sgb = SGB[cg]
                    for st in range(NS):
                        b = psum()
                        tok = slice(st * 128, (st + 1) * 128)
                        mm_group(ps[b][:, :], lambda kc, tok=tok: UT[:, kc, tok], lambda kc, s1=s1: W[s1][:, kc, :],
                                 utk + [("W", s1)], [("ps", b)])
                        P.op("act", lambda e, b=b, sgb=sgb, st=st: e.activation(out=sgb[:, st, :], in_=ps[b][:, :],
                                                                               func=AF.Sigmoid),
                             [("ps", b)], [("SGB", cg, st)])

                for j in range(4):
                    sa = wload(("pa", j))
                    sbk = wload(("pb", j))
                    for q in range(4):
                        c = j * 4 + q
                        g = GLU[c % 2]
                        gk = ("GLU", c % 2)
                        ba, bb, bh = psum(), psum(), psum()
                        cs = slice(q * 128, (q + 1) * 128)
                        mm_group(ps[ba][:, 0:TT], lambda kc, sa=sa, cs=cs: W[sa][:, kc, cs], lambda kc: UT[:, kc, :],
                                 utk + [("W", sa)], [("ps", ba)])
                        mm_group(ps[bb][:, 0:TT], lambda kc, sbk=sbk, cs=cs: W[sbk][:, kc, cs],
                                 lambda kc: UT[:, kc, :], utk + [("W", sbk)], [("ps", bb)])

                        def fh(e, sa=sa, sbk=sbk, cs=cs, bh=bh):
                            ins = None
                            for kc in range(16):
                                ins = e.matmul(ps[bh][:, 0:32], lhsT=W[sa][:, kc, cs], rhs=UH[:, kc, :],
                                               start=(kc == 0), stop=(kc == 15))
                            for kc in range(16):
                                ins = e.matmul(ps[bh][:, 32:64], lhsT=W[sbk][:, kc, cs], rhs=UH[:, kc, :],
                                               start=(kc == 0), stop=(kc == 15))
                            return ins
                        P.op("pe", fh, ["UH", ("W", sa), ("W", sbk)], [("ps", bh)])
                        P.op("act", lambda e, bb=bb: e.activation(out=TS, in_=ps[bb][:, 0:TT], func=AF.Sigmoid),
                             [("ps", bb)], ["TS"])
                        P.op("act", lambda e, bh=bh: e.activation(out=TSH, in_=ps[bh][:, 32:64], func=AF.Sigmoid),
                             [("ps", bh)], ["TSH"])
                        P.op("dve", lambda e, g=g, ba=ba: e.tensor_tensor(out=g[:, CP:CP + TT], in0=ps[ba][:, 0:TT],
                                                                          in1=TS, op=ALU.mult),
                             [("ps", ba), "TS"], [gk])
                        if ti > 0:
                            P.op("dve", lambda e, g=g, bh=bh: e.tensor_tensor(out=g[:, 0:CP], in0=ps[bh][:, 0:CP],
                                                                              in1=TSH[:, 0:CP], op=ALU.mult),
                                 [("ps", bh), "TSH"], [gk])
                        else:
                            P.op("dve", lambda e, g=g: e.memset(g[:, 0:CP], 0.0), [], [gk])
                        P.op("dve", lambda e, g=g, bh=bh: e.tensor_tensor(
                            out=g[:, CP + TT:2 * CP + TT], in0=ps[bh][:, CP:2 * CP], in1=TSH[:, CP:2 * CP],
                            op=ALU.mult), [("ps", bh), "TSH"], [gk])
                        dg = DG[c % 2]
                        dgk = ("DG", c % 2)
                        for k in range(NPE):
                            P.op("act", lambda e, dg=dg, k=k, c=c: e.activation(
                                out=dg[:, k, :], in_=idb[:], func=AF.Copy,
                                scale=pv[:, PV_CW + c * CW + k:PV_CW + c * CW + k + 1]), ["idb", "pv"], [dgk])
                        a = ACC[c % 3]
                        ak = ("ACC", c % 3)
                        a2 = ACC2[0]
                        a2k = ("ACC2", 0)
                        w0, w1 = NPE, NPE + 1
                        P.op("dve", lambda e, g=g, a=a, c=c, w0=w0: e.tensor_scalar(
                            out=a, in0=g[:, w0:w0 + TT], scalar1=pv[:, PV_CW + c * CW + w0:PV_CW + c * CW + w0 + 1],
                            scalar2=pv[:, PV_CB + c:PV_CB + c + 1], op0=ALU.mult, op1=ALU.add),
                            [gk, "pv"], [ak])
                        P.op("dve", lambda e, g=g, a2=a2, c=c, w1=w1: e.tensor_scalar(
                            out=a2, in0=g[:, w1:w1 + TT], scalar1=pv[:, PV_CW + c * CW + w1:PV_CW + c * CW + w1 + 1],
                            scalar2=None, op0=ALU.mult), [gk, "pv"], [a2k])
                        for i_, w in enumerate(range(NPE + 2, CW)):
                            tg, tgk = (a, ak) if i_ % 2 == 0 else (a2, a2k)
                            P.op("dve", lambda e, g=g, tg=tg, c=c, w=w: e.scalar_tensor_tensor(
                                out=tg, in0=g[:, w:w + TT], scalar=pv[:, PV_CW + c * CW + w:PV_CW + c * CW + w + 1],
                                in1=tg, op0=ALU.mult, op1=ALU.add), [gk, tgk, "pv"], [tgk])
                        P.op("dve", lambda e, a=a, a2=a2: e.tensor_tensor(out=a, in0=a, in1=a2, op=ALU.add),
                             [ak, a2k], [ak])
                        def conv_pe(c=c, dg=dg, dgk=dgk, g=g, gk=gk, a=a, ak=ak):
                            bc = psum()

                            def fconv(e):
                                ins = None
                                for k in range(NPE):
                                    ins = e.matmul(ps[bc][:, 0:TT], lhsT=dg[:, k, :], rhs=g[:, k:k + TT],
                                                   start=(k == 0), stop=(k == NPE - 1))
                                return ins
                            P.op("pe", fconv, [dgk, gk], [("ps", bc)])
                            P.op("dve", lambda e: e.tensor_tensor(out=a, in0=ps[bc][:, 0:TT], in1=a, op=ALU.add),
                                 [ak, ("ps", bc)], [ak])
                            pending_stats.append(c)
                        if pending_conv:
                            pending_conv.pop(0)()
                        pending_conv.append(conv_pe)
                        if len(pending_stats) > 1:
                            stats(pending_stats.pop(0))
                while pending_conv:
                    pending_conv.pop(0)()
                while pending_stats:
                    stats(pending_stats.pop(0))
                for cg_ in range(4):
                    gate_b_block(cg_)

                stop_at("T2")
                P.op("dve", lambda e: e.tensor_copy(out=MEAN, in_=ps[6][:, 0:TT]), [("ps", 6)], ["MEAN"])
                P.op("dve", lambda e: e.tensor_tensor(out=NMR, in0=MEAN, in1=MEAN, op=ALU.mult), ["MEAN"], ["NMR"])
                P.op("dve", lambda e: e.tensor_tensor(out=RSTD, in0=ps[7][:, 0:TT], in1=NMR, op=ALU.subtract),
                     [("ps", 7), "NMR"], ["RSTD"])
                P.op("act", lambda e: e.activation(out=RSTD, in_=RSTD, func=AF.Ln, bias=EPS), ["RSTD"], ["RSTD"])
                P.op("act", lambda e: e.activation(out=RSTD, in_=RSTD, func=AF.Exp, scale=-0.5), ["RSTD"], ["RSTD"])
                P.op("dve", lambda e: e.scalar_tensor_tensor(out=NMR, in0=MEAN, scalar=-1.0, in1=RSTD, op0=ALU.mult,
                                                             op1=ALU.mult), ["MEAN", "RSTD"], ["NMR"])
                for c in range(16):
                    t = TMPT[c % 2]
                    tk = ("TMPT", 0)
                    P.op("dve", lambda e, c=c, t=t: e.tensor_tensor(out=t, in0=DWB[:, c, :], in1=RSTD, op=ALU.mult),
                         [("DWB", c), "RSTD"], [tk])
                    P.op("dve", lambda e, t=t: e.tensor_tensor(out=t, in0=t, in1=NMR, op=ALU.add),
                         [tk, "NMR"], [tk])
                    P.op("act", lambda e, c=c, t=t: e.activation(
                        out=ACTT[:, c, :], in_=t, func=AF.Silu, scale=pv[:, PV_LG + c:PV_LG + c + 1],
                        bias=pv[:, PV_LB + c:PV_LB + c + 1]), [tk, "pv"], [("XS", c * NS // 16)])
                actk = [("XS", s_) for s_ in range(NS)]

                stop_at("T3")
                for cg in range(4):
                    sgb = SGB[cg]
                    s2 = wload(("pw2", cg))
                    for st in range(NS):
                        b = psum()
                        tok = slice(st * 128, (st + 1) * 128)
                        mm_group(ps[b][:, :], lambda kc, tok=tok: ACTT[:, kc, tok], lambda kc, s2=s2: W[s2][:, kc, :],
                                 actk + [("W", s2)], [("ps", b)])
                        P.op("dve", lambda e, b=b, sgb=sgb, st=st, cg=cg: e.tensor_tensor(
                            out=MB[:, st, cg * 512:(cg + 1) * 512], in0=ps[b][:, :], in1=sgb[:, st, :], op=ALU.mult),
                            [("ps", b), ("SGB", cg, st)], [("MB", st, cg)])

                stop_at("T4")
                P.dma("sp", "rb", lambda e: e.dma_start(out=RB[:], in_=rvec_d[0, :].partition_broadcast(128)),
                      writes=["RB"])
                P.barrier(INTILE)
                def gla_head(h):
                    hk = lambda n: ("H", n)
                    sA = wload(("qk", h))
                    for half in range(2):
                        b = psum()
                        cs = slice(half * 128, (half + 1) * 128)
                        mm_group(ps[b][:, 0:TT], lambda kc, cs=cs: W[sA][:, kc, cs], lambda kc: UT[:, kc, :],
                                 utk + [("W", sA)], [("ps", b)])
                        ecopy(evac_eng(), QT[:, half, :], ps[b][:, 0:TT], [("ps", b)], [hk("QT")], scale=HDK ** -0.5)
                    c0 = cbase + ti * NS
                    P.dma("sp", "kkl", lambda e, c0=c0, h=h: e.dma_start(
                        out=KK[:, :, :], in_=kvscr[c0:c0 + NS, :, h * HDK:(h + 1) * HDK].rearrange("s p c -> p s c")),
                        reads=[("kvk", c0 + i_) for i_ in range(NS)], writes=[hk("KK")])
                    P.dma("sp", "vvl", lambda e, c0=c0, h=h: e.dma_start(
                        out=VV[:, :, :],
                        in_=kvscr[c0:c0 + NS, :, DK + h * HDV:DK + (h + 1) * HDV].rearrange("s p c -> p s c")),
                        reads=[("kvv", c0 + i_) for i_ in range(NS)], writes=[hk("VV")])
                    b = psum()
                    pbk = ps[b][:].bitcast(BF16)

                    def fkt(e, pbk=pbk):
                        ins = None
                        for half in range(2):
                            for st in range(NS):
                                ins = e.transpose(out=pbk[:, half * TT + st * 128:half * TT + (st + 1) * 128],
                                                  in_=KK[:, st, half * 128:(half + 1) * 128], identity=idb[:])
                        return ins
                    P.op("pe", fkt, [hk("KK"), "idb"], [("ps", b)])
                    ecopy("act", KT[:].rearrange("p a b -> p (a b)"), pbk[:, 0:2 * TT], [("ps", b)], [hk("KT")])
                    def emit_gates():
                      sG = wload(("g", h))
                      for st in range(NS):
                          b = psum()
                          tok = slice(st * 128, (st + 1) * 128)
                          mm_group(ps[b][:, :], lambda kc, tok=tok: UT[:, kc, tok], lambda kc: W[sG][:, kc, :],
                                   utk + [("W", sG)], [("ps", b)])
                          P.op("act", lambda e, b=b, st=st: e.activation(out=G1[:, st, :], in_=ps[b][:, :],
                                                                         func=AF.Silu), [("ps", b)], [hk("G1")])
                      sGA = wload(("ga", h))
                      for st in range(NS):
                          b = psum()
                          tok = slice(st * 128, (st + 1) * 128)
                          mm_group(ps[b][:, :], lambda kc, tok=tok: UT[:, kc, tok], lambda kc: W[sGA][:, kc, :],
                                   utk + [("W", sGA)], [("ps", b)])
                          P.op("act", lambda e, b=b, st=st: e.activation(out=G2[st % 2], in_=ps[b][:, :],
                                                                         func=AF.Sigmoid), [("ps", b)],
                               [("G2", 0)])
                          P.op("dve", lambda e, st=st: e.tensor_tensor(out=G1[:, st, :], in0=G1[:, st, :],
                                                                       in1=G2[st % 2], op=ALU.mult),
                               [hk("G1"), ("G2", 0)], [hk("G1")])
                      for st in range(NS):
                          P.op("dve", lambda e, st=st, h=h: e.tensor_tensor(
                              out=GG[:, st, :], in0=G1[:, st, :], in1=RB[:, h * HDV:(h + 1) * HDV], op=ALU.mult),
                              [hk("G1"), "RB"], [hk("G1")])
                    stop_at("T4a")
                    def emit_decays():
                      for st in range(NS):
                          b = psum()
                          tok = slice(st * 128, (st + 1) * 128)

                          P.op("pe", lambda e, b=b, tok=tok, h=h: e.matmul(
                              ps[b][:, :], lhsT=zT[0:49, tok], rhs=wa_sb[0:49, h * 512:(h + 1) * 512],
                              start=True, stop=True), ["zT", "wa"], [("ps", b)])
                          P.op("act", lambda e, b=b: e.activation(out=SPX, in_=ps[b][:, :], func=AF.Exp, scale=-1.0),
                               [("ps", b)], ["SPX"])
                          P.op("act", lambda e, st=st: e.activation(out=SP[:, st, :], in_=SPX, func=AF.Ln, bias=1.0),
                               ["SPX"], [hk("SP")])
                    stop_at("T4b")

                    def chunk_x(st):
                        c = cbase + ti * NS + st
                        r = st % 2
                        ck = lambda n, r=r: ("C", n, r)
                        tok = slice(st * 128, (st + 1) * 128)
                        s2 = (h * NS + st) % 3
                        P.dma("sp", f"s2l{s2}", lambda e, s2=s2, c=c, h=h: e.dma_start(
                            out=S2L[s2][:], in_=s2scr[c, h, :, :, :]), reads=[("s2", c, h)], writes=[("S2L", s2)])
                        bB, bQ, bK = psum(), psum(), psum()

                        def fB(e, bB=bB, bQ=bQ, bK=bK, st=st):
                            ins = None
                            for d in range(2):
                                for half in range(2):
                                    o = (d * 2 + half) * 128
                                    lh = SP[:, st, d * 256 + half * 128:d * 256 + (half + 1) * 128]
                                    e.matmul(ps[bB][:, o:o + 128], lhsT=lh, rhs=CM(CM_B1 + d), start=True, stop=True)
                                    e.matmul(ps[bQ][:, o:o + 128], lhsT=lh, rhs=CM(CM_BQ + d), start=True, stop=True)
                            ins = e.matmul(ps[bK][:, 0:HDK], lhsT=CM(CM_KD + 0), rhs=SP[:, st, 0:HDK],
                                           start=True, stop=True)
                            return ins
                        P.op("pe", fB, [hk("SP"), "cm"], [("ps", bB), ("ps", bQ), ("ps", bK)])
                        P.op("act", lambda e, bB=bB, r=r: e.activation(out=EB[r], in_=ps[bB][:, :], func=AF.Exp),
                             [("ps", bB)], [ck("EB")])
                        P.op("act", lambda e, bQ=bQ, r=r: e.activation(out=EQ[r], in_=ps[bQ][:, :], func=AF.Exp),
                             [("ps", bQ)], [ck("EQ")])
                        P.op("act", lambda e, bQ=bQ, r=r: e.activation(out=EK[r], in_=ps[bQ][:, :], func=AF.Exp,
                                                                       scale=-1.0), [("ps", bQ)], [ck("EK")])
                        P.op("act", lambda e, bK=bK, r=r: e.activation(out=EKD[r], in_=ps[bK][:, 0:HDK],
                                                                       func=AF.Exp), [("ps", bK)], [ck("EKD")])
                        stop_at("T4c")

                    def chunk_x2(st):
                        r = st % 2
                        ck = lambda n, r=r: ("C", n, r)
                        tok = slice(st * 128, (st + 1) * 128)
                        for d in range(2):
                            ev = lambda t, d=d: t[:, d * 256:(d + 1) * 256].rearrange("p (a b) -> p a b", a=2)
                            o3 = lambda t, d=d: t[:, d, :].rearrange("p (a b) -> p a b", a=2)
                            P.op("dve", lambda e, r=r, d=d, ev=ev, o3=o3, tok=tok: e.tensor_tensor(
                                out=o3(Q1[r]), in0=QT[:, :, tok], in1=ev(EB[r]), op=ALU.mult),
                                [hk("QT"), ck("EB")], [ck("Q1")])
                            P.op("dve", lambda e, r=r, d=d, ev=ev, o3=o3, tok=tok: e.tensor_tensor(
                                out=o3(Q2[r]), in0=QT[:, :, tok], in1=ev(EQ[r]), op=ALU.mult),
                                [hk("QT"), ck("EQ")], [ck("Q2")])
                            P.op("dve", lambda e, r=r, d=d, ev=ev, o3=o3, tok=tok: e.tensor_tensor(
                                out=o3(K2[r]), in0=KT[:, :, tok], in1=ev(EK[r]), op=ALU.mult),
                                [hk("KT"), ck("EK")], [ck("K2")])
                        P.op("dve", lambda e, r=r, st=st: e.tensor_tensor(out=KD[r], in0=KK[:, st, :], in1=EKD[r],
                                                                          op=ALU.mult),
                             [hk("KK"), ck("EKD")], [ck("KD")])
                        stop_at("T4d")

                    def chunk_y(st, mid=None):
                        r = st % 2
                        ck = lambda n, r=r: ("C", n, r)
                        s2 = (h * NS + st) % 3
                        assert NS % 2 == 0
                        cur_s, cur_k = (Sb[0][:, h, :, :], ("Sb", 0, h)) if st % 2 == 0 else (SBT, "SBT")
                        nxt_s, nxt_k = (SBT, "SBT") if st % 2 == 0 else (Sb[0][:, h, :, :], ("Sb", 0, h))
                        for half in range(2):
                            b = psum()
                            P.op("pe", lambda e, b=b, r=r, half=half, st=st: e.matmul(
                                ps[b][:, :], lhsT=KD[r][:, half * 128:(half + 1) * 128], rhs=VV[:, st, :],
                                start=True, stop=True), [ck("KD"), hk("VV")], [("ps", b)])
                            P.op("dve", lambda e, b=b, r=r, half=half, h=h: e.scalar_tensor_tensor(
                                out=Sf[:, h, half, :], in0=Sf[:, h, half, :],
                                scalar=EB[r][:, half * 128 + 127:half * 128 + 128], in1=ps[b][:, :],
                                op0=ALU.mult, op1=ALU.add), [("ps", b), ck("EB"), ("Sf", h)], [("Sf", h)])
                        P.op("act", lambda e, h=h, nxt_s=nxt_s: e.activation(out=nxt_s, in_=Sf[:, h, :, :],
                                                                             func=AF.Copy), [("Sf", h)], [nxt_k])
                        bS = psum()

                        def fS(e, bS=bS, r=r):
                            ins = None
                            for d in range(2):
                                for half in range(2):
                                    ins = e.matmul(ps[bS][:, d * 128:(d + 1) * 128],
                                                   lhsT=K2[r][:, d, half * 128:(half + 1) * 128],
                                                   rhs=Q2[r][:, d, half * 128:(half + 1) * 128],
                                                   start=(half == 0), stop=(half == 1))
                            return ins
                        P.op("pe", fS, [ck("K2"), ck("Q2")], [("ps", bS)])
                        P.op("dve", lambda e, bS=bS, r=r: e.tensor_tensor(
                            out=PM[r], in0=ps[bS][:, 0:256], in1=CM(CM_MASK, 2), op=ALU.mult),
                            [("ps", bS), "cm"], [ck("PM")])
                        stop_at("T4e")
                        if mid is not None:
                            mid()
                        bO = psum()

                        def fO(e, bO=bO, r=r, st=st, h=h, s2=s2, cur_s=cur_s):
                            e.matmul(ps[bO][:, :], lhsT=Q1[r][:, 0, 0:128], rhs=cur_s[:, 0, :], start=True, stop=False)
                            e.matmul(ps[bO][:, :], lhsT=Q1[r][:, 0, 128:256], rhs=cur_s[:, 1, :], start=False,
                                     stop=False)
                            e.matmul(ps[bO][:, :], lhsT=PM[r][:, 0:128], rhs=VV[:, st, :], start=False, stop=False)
                            e.matmul(ps[bO][:, :], lhsT=Q1[r][:, 1, 0:128], rhs=S2L[s2][:, 0, :], start=False,
                                     stop=False)
                            e.matmul(ps[bO][:, :], lhsT=Q1[r][:, 1, 128:256], rhs=S2L[s2][:, 1, :], start=False,
                                     stop=False)
                            return e.matmul(ps[bO][:, :], lhsT=PM[r][:, 128:256], rhs=VV[:, st, :], start=False,
                                            stop=True)
                        P.op("pe", fO, [ck("Q1"), ck("PM"), hk("VV"), cur_k, ("S2L", s2)], [("ps", bO)])
                        stop_at("T4f")
                        col = 9 + r
                        P.op("act", lambda e, bO=bO, col=col: e.activation(out=OJ, in_=ps[bO][:, :], func=AF.Square,
                                                                           accum_out=ss_t[:, col:col + 1]),
                             [("ps", bO)], ["SPX", ("ss", col)])
                        rstd_from_ss(col, HDV)
                        P.op("dve", lambda e, bO=bO, col=col, r=r, st=st: e.scalar_tensor_tensor(
                            out=OT[r], in0=ps[bO][:, :], scalar=rs_t[:, col:col + 1], in1=GG[:, st, :],
                            op0=ALU.mult, op1=ALU.mult), [("ps", bO), ("rs", col), hk("G1")], [ck("OT")])
                        P.op("dve", lambda e, r=r, st=st, h=h: e.tensor_tensor(
                            out=MB[:, st, h * HDV:(h + 1) * HDV], in0=MB[:, st, h * HDV:(h + 1) * HDV], in1=OT[r],
                            op=ALU.add), [ck("OT"), ("MB", st, h)], [("MB", st, h)])
                        stop_at("T4g")
                    emit_decays()
                    chunk_x(0)
                    chunk_x2(0)
                    if NS > 1:
                        chunk_x(1)
                        chunk_x2(1)
                    emit_gates()
                    for st in range(NS):
                        chunk_y(st, (lambda st=st: chunk_x(st + 2)) if st + 2 < NS else None)
                        if st + 2 < NS:
                            chunk_x2(st + 2)

                for h_ in range(NH):
                    gla_head(h_)
                stop_at("T5")
                P.barrier(INTILE)
                mbk = [("MB", st, q) for st in range(NS) for q in range(4)]
                transposes_to_UT(lambda st: MB[:, st, :], mbk, None)
                P.dma("sp", "rb", lambda e: e.dma_start(out=RB[:], in_=rvec_d[1, :].partition_broadcast(128)),
                      writes=["RB"])
                for cg in range(4):
                    s = wload(("wo", cg))
                    for st in range(NS):
                        b = psum()
                        tok = slice(st * 128, (st + 1) * 128)
                        mm_group(ps[b][:, :], lambda kc, tok=tok: UT[:, kc, tok], lambda kc, s=s: W[s][:, kc, :],
                                 utk + [("W", s)], [("ps", b)])
                        ecopy(evac_eng(), FB[:, st, cg * 512:(cg + 1) * 512], ps[b][:, :],
                              [("ps", b)], [("FB", st)] + mbk)

                def post_norm_residual(st, res_load, store, final=False):
                    if final:
                        jk = AT[0][:].rearrange("p a b -> p (a b)")[:, 0:D]
                        jkeys = [("AT", 0, i_) for i_ in range(D // TT)]
                        col = 11 + st
                    else:
                        jk, jkeys, col = XS[:, st, :], [("XS", st)], st
                    P.op("act", lambda e, st=st, jk=jk, col=col: e.activation(
                        out=jk, in_=FB[:, st, :], func=AF.Square, accum_out=ss_t[:, col:col + 1]),
                        [("FB", st)], jkeys + [("ss", col)])
                    rstd_from_ss(col, D)
                    xs_ = xtrot[0] % 2
                    xtrot[0] += 1
                    res_load(st, xs_)
                    P.op("dve", lambda e, st=st, col=col: e.scalar_tensor_tensor(
                        out=FB[:, st, :], in0=FB[:, st, :], scalar=rs_t[:, col:col + 1], in1=RB[:], op0=ALU.mult,
                        op1=ALU.mult), [("FB", st), ("rs", col), "RB"], [("FB", st)])
                    P.op("dve", lambda e, st=st, xs_=xs_: e.tensor_tensor(out=FB[:, st, :], in0=FB[:, st, :],
                                                                          in1=XT[xs_][:], op=ALU.add),
                         [("FB", st), ("XT", xs_)], [("FB", st)])
                    store(st)

                def load_x(st, xs_):
                    r0 = row0 + st * 128
                    P.dma("sp", f"XT{xs_}", lambda e, xs_=xs_, r0=r0: e.dma_start(out=XT[xs_][:], in_=x[r0:r0 + 128, :]),
                          writes=[("XT", xs_)])

                def store_h(st):
                    P.dma("sp", "hst", lambda e, st=st: e.dma_start(out=hscr[st, :, :], in_=FB[:, st, :]),
                          reads=[("FB", st)], writes=[("hscr", st)])

                for st in range(NS):
                    post_norm_residual(st, load_x, store_h)

                stop_at("T6")
                for st in range(NS):
                    P.op("act", lambda e, st=st: e.activation(out=XS[:, st, :], in_=FB[:, st, :], func=AF.Square,
                                                              accum_out=ss_t[:, 4 + st:5 + st]),
                         [("FB", st)], [("XS", st), ("ss", 4 + st)])
                    rstd_from_ss(4 + st, D)
                    P.op("dve", lambda e, st=st: e.tensor_scalar(out=XS[:, st, :], in0=FB[:, st, :],
                                                                  scalar1=rs_t[:, 4 + st:5 + st], scalar2=None,
                                                                  op0=ALU.mult),
                         [("FB", st), ("rs", 4 + st)], [("XS", st)])
                transposes_to_UT(lambda st: XS[:, st, :], [("XS", st) for st in range(NS)], PV_GFFN)
                if ti + 1 < ntile:
                    x_load(x, row0 + TT, XS)

                stop_at("T7")
                for kb in range(4):
                    at = AT[kb % 2]
                    for j in range(4):
                        s = wload(("ff1", kb * 4 + j))
                        for q in range(4):
                            b = psum()
                            cs = slice(q * 128, (q + 1) * 128)
                            mm_group(ps[b][:, 0:TT], lambda kc, s=s, cs=cs: W[s][:, kc, cs], lambda kc: UT[:, kc, :],
                                     utk + [("W", s)], [("ps", b)])
                            rl = RL[(j * 4 + q) % 2]
                            rk = ("RL", (j * 4 + q) % 2)
                            P.op("act", lambda e, b=b, rl=rl: e.activation(out=rl, in_=ps[b][:, 0:TT], func=AF.Relu),
                                 [("ps", b)], [rk])
                            P.op("dve", lambda e, rl=rl, at=at, j=j, q=q: e.tensor_tensor(
                                out=at[:, j * 4 + q, :], in0=rl, in1=rl, op=ALU.mult), [rk], [("AT", kb % 2, j * 4 + q)])
                    atk = [("AT", kb % 2, i) for i in range(16)]
                    for cg in range(4):
                        s = wload(("ff2", kb, cg))
                        for st in range(NS):
                            b = psum()
                            tok = slice(st * 128, (st + 1) * 128)
                            mm_group(ps[b][:, :], lambda kc, tok=tok, at=at: at[:, kc, tok],
                                     lambda kc, s=s: W[s][:, kc, :], atk + [("W", s)], [("ps", b)])
                            dst = FB[:, st, cg * 512:(cg + 1) * 512]
                            if kb == 0:
                                ecopy("act", dst, ps[b][:, :], [("ps", b), ("hscr", st)], [("FB", st)])
                            else:
                                P.op("dve", lambda e, b=b, dst=dst: e.tensor_tensor(out=dst, in0=dst, in1=ps[b][:, :],
                                                                                    op=ALU.add),
                                     [("ps", b), ("FB", st)], [("FB", st)])
                P.dma("sp", "rb", lambda e: e.dma_start(out=RB[:], in_=rvec_d[2, :].partition_broadcast(128)),
                      writes=["RB"])

                def load_h(st, xs_):
                    P.dma("sp", f"XT{xs_}", lambda e, xs_=xs_, st=st: e.dma_start(out=XT[xs_][:], in_=hscr[st, :, :]),
                          reads=[("hscr", st)], writes=[("XT", xs_)])

                def store_y(st):
                    r0 = row0 + st * 128
                    P.dma("sp", "yst", lambda e, st=st, r0=r0: e.dma_start(out=y[r0:r0 + 128, :], in_=FB[:, st, :]),
                          reads=[("FB", st)], writes=[("y", sg["name"], r0)])

                for st in range(NS):
                    post_norm_residual(st, load_h, store_y, final=True)

            for ti in range(ntile):
                main_tile(ti)

        for sg in segs:
            main_seg(sg)

    for sg in segs:
        sg["sbcur"] = None
    try:
        stop_at("const")
        presweep()
        P.barrier(skip=("d_cv",))
        stop_at("pre")
        mainsweep()
    except _Stop:
        pass
    P.barrier()
    P.emit()
    es.close()
    return nc


def _pvec(norm_mix_pre, norm_ffn_pre, conv_b, conv_ln_g, conv_ln_b, conv_w_local):
    t = lambda v: np.asarray(v, np.float32).reshape(16, 128).T
    cw = np.asarray(conv_w_local, np.float32).reshape(CW, 16, 128).transpose(2, 1, 0).reshape(128, 16 * CW)
    return np.ascontiguousarray(np.concatenate(
        [t(norm_mix_pre), t(norm_ffn_pre), t(conv_b), t(conv_ln_g), t(conv_ln_b), cw], axis=1))


_NC_CACHE = {}


def kernel(x_prompt, x_sample, norm_mix_pre, norm_mix_post, norm_ffn_pre, norm_ffn_post,
           w_in, w_a_fwd, b_a_fwd, w_a_bwd, b_a_bwd, gla_norm, conv_w, conv_b,
           conv_ln_g, conv_ln_b, w_pw2, w_out, w_ff1, w_ff2, _NS=4):
    x_prompt = np.asarray(x_prompt, np.float32)
    x_sample = np.asarray(x_sample, np.float32)
    B, SP_, _ = x_prompt.shape
    _, SS_, _ = x_sample.shape
    n_cores = 2 * B
    key = (SP_, SS_, _NS)
    if key not in _NC_CACHE:
        _NC_CACHE[key] = build_program(SP_, SS_, _NS)
    nc = _NC_CACHE[key]
    w_in = np.ascontiguousarray(np.asarray(w_in, np.float32))
    cmat = make_cmat()
    rvec = np.ascontiguousarray(np.stack([np.asarray(gla_norm, np.float32), np.asarray(norm_mix_post, np.float32),
                                          np.asarray(norm_ffn_post, np.float32)]))
    zf = w_in[:, OFF_ZF:OFF_ZF + 16]
    zb = w_in[:, OFF_ZB:OFF_ZB + 16]
    waf = np.concatenate([np.asarray(w_a_fwd, np.float32), np.asarray(b_a_fwd, np.float32)[None, :]], 0)
    wab = np.concatenate([np.asarray(w_a_bwd, np.float32), np.asarray(b_a_bwd, np.float32)[None, :]], 0)
    conv_w = np.asarray(conv_w, np.float32)
    shared = dict(w_in=w_in, w_pw2=np.ascontiguousarray(w_pw2, np.float32),
                  w_out=np.ascontiguousarray(w_out, np.float32), w_ff1=np.ascontiguousarray(w_ff1, np.float32),
                  w_ff2=np.ascontiguousarray(w_ff2, np.float32), rvec=rvec, cmat=cmat)
    per_half = []
    for half in range(2):
        if half == 0:
            wz = np.concatenate([zf, zb], 1)
            wa = np.stack([waf, wab], 1)
            cwl = conv_w
        else:
            wz = np.concatenate([zb, zf], 1)
            wa = np.stack([wab, waf], 1)
            cwl = conv_w[::-1]
        per_half.append(dict(wz=np.ascontiguousarray(wz), wa=np.ascontiguousarray(wa),
                             pvec=_pvec(norm_mix_pre, norm_ffn_pre, conv_b, conv_ln_g, conv_ln_b, cwl)))
    in_maps = []
    for c in range(n_cores):
        p, half = c // 2, c % 2
        xp = x_prompt[p] if half == 0 else x_prompt[p][::-1]
        xs = x_sample[p] if half == 0 else x_sample[p][::-1]
        m = dict(shared)
        m.update(per_half[half])
        m["xp"] = np.ascontiguousarray(xp)
        m["xs"] = np.ascontiguousarray(xs)
        in_maps.append(m)
    res = run_bass_kernel_spmd(nc, in_maps, core_ids=list(range(n_cores)))
    y_p = np.empty((B, SP_, D), np.float32)
    y_s = np.empty((B, SS_, D), np.float32)
    for c in range(n_cores):
        p, half = c // 2, c % 2
        r = res.results[c]
        yp, ys = np.asarray(r["yp"]), np.asarray(r["ys"])
        if half == 0:
            y_p[p, :SP_ // 2] = yp
            y_s[p, :SS_ // 2] = ys
        else:
            y_p[p, SP_ // 2:] = yp[::-1]
            y_s[p, SS_ // 2:] = ys[::-1]
    return (y_p, y_s)
```
